# Optimizing a Trainium2 kernel written in Bass

```python
import math
import jax, jax.numpy as jnp
from jax import lax
import numpy as np


D_MODEL = 1024
BATCH = 8
SEQ = 4096
DEPTH = 2

N_MIXERS = 2
N_A_LAYERS = (DEPTH + 1) // 2
N_B_LAYERS = DEPTH // 2
Q_BLOCK = 128

SB_HEADS = 16
SB_HEAD_DIM = D_MODEL // SB_HEADS

DSA_HEADS = 16
DSA_LATENT = 128
DSA_V_DIM = D_MODEL // DSA_HEADS
IDX_HEADS = 8
IDX_DIM = 64
TOPK_MAX = 256
DSA_IN = DSA_HEADS * DSA_LATENT + DSA_LATENT + IDX_HEADS * IDX_DIM + IDX_DIM + IDX_HEADS

NUM_BUCKETS = 32
MAX_DISTANCE = 128

FFN_DIM = 2816
CONV_W = 3

RMS_EPS = 1e-6
NEG = -1e30

kernel_name = 'hybrid_stickbreak_dsa_convffn_adaln'


def rms_norm(x, g):
    xf = x.astype(jnp.float32)
    y = xf * lax.rsqrt(jnp.mean(xf * xf, axis=-1, keepdims=True) + RMS_EPS)
    return (y * g.astype(jnp.float32)).astype(x.dtype)


def _blocks(a):
    B, S = a.shape[0], a.shape[1]
    a = a.reshape((B, S // Q_BLOCK, Q_BLOCK) + a.shape[2:])
    return jnp.moveaxis(a, 1, 0)


def _unblocks(a):
    a = jnp.moveaxis(a, 0, 1)
    return a.reshape((a.shape[0], a.shape[1] * a.shape[2]) + a.shape[3:])


def t5_bucket(dist):
    n = jnp.maximum(dist, 0)
    max_exact = NUM_BUCKETS // 2
    nf = jnp.maximum(n, 1).astype(jnp.float32)
    large = max_exact + (jnp.log(nf / max_exact) / math.log(MAX_DISTANCE / max_exact)
                         * (NUM_BUCKETS - max_exact)).astype(jnp.int32)
    large = jnp.minimum(large, NUM_BUCKETS - 1)
    return jnp.where(n < max_exact, n, large)


def stick_breaking_attention(h, w_in, w_out):
    B, S, _ = h.shape
    qkv = h @ w_in
    q, k, v = jnp.split(qkv, 3, axis=-1)
    q = q.reshape(B, S, SB_HEADS, SB_HEAD_DIM).astype(jnp.float32) * (SB_HEAD_DIM ** -0.5)
    k = k.reshape(B, S, SB_HEADS, SB_HEAD_DIM).astype(jnp.float32)
    v = v.reshape(B, S, SB_HEADS, SB_HEAD_DIM).astype(jnp.float32)
    key_pos = jnp.arange(S)

    def block(args):
        qb, t0 = args
        t = t0 + jnp.arange(Q_BLOCK)
        z = jnp.einsum('bqhd,bshd->bhqs', qb, k)
        strict = key_pos[None, :] < t[:, None]
        log_keep = jnp.where(strict, jax.nn.log_sigmoid(-z), 0.0)
        suffix = lax.cumsum(log_keep, axis=3, reverse=True) - log_keep
        a = jnp.where(strict, jnp.exp(jax.nn.log_sigmoid(z) + suffix), 0.0)
        return jnp.einsum('bhqs,bshd->bqhd', a, v)

    starts = jnp.arange(S // Q_BLOCK) * Q_BLOCK
    o = _unblocks(lax.map(block, (_blocks(q), starts)))
    return o.reshape(B, S, SB_HEADS * SB_HEAD_DIM).astype(h.dtype) @ w_out


def dsa_attention(h, w_in, q_gain, k_gain, w_uv, w_out, rel_bias):
    B, S, _ = h.shape
    topk = min(TOPK_MAX, S // 4)
    proj = h @ w_in
    o1 = DSA_HEADS * DSA_LATENT
    o2 = o1 + DSA_LATENT
    o3 = o2 + IDX_HEADS * IDX_DIM
    o4 = o3 + IDX_DIM
    q, lat, qi, ki, wi = jnp.split(proj, [o1, o2, o3, o4], axis=-1)
    q = rms_norm(q.reshape(B, S, DSA_HEADS, DSA_LATENT), q_gain).astype(jnp.float32)
    k = rms_norm(lat, k_gain).astype(jnp.float32)
    vals = lat.astype(jnp.float32)
    qi = qi.reshape(B, S, IDX_HEADS, IDX_DIM).astype(jnp.float32)
    ki = ki.astype(jnp.float32)
    wi = wi.astype(jnp.float32) * (IDX_HEADS ** -0.5)
    bidx = jnp.arange(B)[:, None, None]
    key_pos = jnp.arange(S)
    scale = DSA_LATENT ** -0.5

    def block(args):
        qb, qib, wib, t0 = args
        t = t0 + jnp.arange(Q_BLOCK)
        isc = jax.nn.relu(jnp.einsum('bqhd,bsd->bqhs', qib, ki))
        isc = jnp.einsum('bqhs,bqh->bqs', isc, wib)
        causal = key_pos[None, :] <= t[:, None]
        isc = jnp.where(causal[None], isc, NEG)
        _, sel = lax.top_k(isc, topk)
        valid = sel <= t[None, :, None]
        kg = k[bidx, sel]
        vg = vals[bidx, sel]
        logits = jnp.einsum('bqhd,bqkd->bqhk', qb, kg) * scale
        bias = rel_bias[t5_bucket(t[None, :, None] - sel)]
        logits = logits + jnp.swapaxes(bias, -1, -2).astype(jnp.float32)
        logits = jnp.where(valid[:, :, None, :], logits, NEG)
        p = jax.nn.softmax(logits, axis=-1)
        return jnp.einsum('bqhk,bqkd->bqhd', p, vg)

    starts = jnp.arange(S // Q_BLOCK) * Q_BLOCK
    o_lat = _unblocks(lax.map(block, (_blocks(q), _blocks(qi), _blocks(wi), starts)))
    o = jnp.einsum('bshl,hlv->bshv', o_lat, w_uv.astype(jnp.float32))
    return o.reshape(B, S, DSA_HEADS * DSA_V_DIM).astype(h.dtype) @ w_out


def conv_ffn(h, w_up, conv_w, conv_b, w_down):
    S = h.shape[1]
    u = h @ w_up
    up = jnp.pad(u, ((0, 0), (CONV_W - 1, 0), (0, 0)))
    y = conv_b
    for j in range(CONV_W):
        y = y + up[:, j:j + S] * conv_w[j]
    gate, val = jnp.split(y, 2, axis=-1)
    return (jax.nn.silu(gate) * val) @ w_down


def setup_inputs(seed: int = 0) -> dict:
    key = jax.random.key(seed)
    ks = jax.random.split(key, 20)
    f32 = jnp.float32
    D = D_MODEL

    def nrm(k, shape, s):
        return jax.random.normal(k, shape, f32) * s

    return {
        'x': nrm(ks[0], (BATCH, SEQ, D), 1.0),
        'c': nrm(ks[1], (BATCH, D), 1.0),
        'ada_w': nrm(ks[2], (DEPTH, D, 6 * D), 0.5 * D ** -0.5),
        'ada_b': nrm(ks[3], (DEPTH, 6 * D), 0.01),
        'norm_mix': 1.0 + nrm(ks[4], (DEPTH, D), 0.05),
        'norm_ffn': 1.0 + nrm(ks[5], (DEPTH, D), 0.05),
        'sb_w_in': nrm(ks[6], (N_A_LAYERS, D, 3 * SB_HEADS * SB_HEAD_DIM), D ** -0.5),
        'sb_w_out': nrm(ks[7], (N_A_LAYERS, SB_HEADS * SB_HEAD_DIM, D), (SB_HEADS * SB_HEAD_DIM) ** -0.5),
        'dsa_w_in': nrm(ks[8], (N_B_LAYERS, D, DSA_IN), D ** -0.5),
        'dsa_q_norm': 1.0 + nrm(ks[9], (N_B_LAYERS, DSA_LATENT), 0.05),
        'dsa_k_norm': 1.0 + nrm(ks[10], (N_B_LAYERS, DSA_LATENT), 0.05),
        'dsa_w_uv': nrm(ks[11], (N_B_LAYERS, DSA_HEADS, DSA_LATENT, DSA_V_DIM), DSA_LATENT ** -0.5),
        'dsa_w_out': nrm(ks[12], (N_B_LAYERS, DSA_HEADS * DSA_V_DIM, D), (DSA_HEADS * DSA_V_DIM) ** -0.5),
        'rel_bias': nrm(ks[13], (NUM_BUCKETS, DSA_HEADS), 0.5),
        'ffn_w_up': nrm(ks[14], (DEPTH, D, 2 * FFN_DIM), D ** -0.5),
        'ffn_conv_w': nrm(ks[15], (DEPTH, CONV_W, 2 * FFN_DIM), CONV_W ** -0.5),
        'ffn_conv_b': nrm(ks[16], (DEPTH, 2 * FFN_DIM), 0.01),
        'ffn_w_down': nrm(ks[17], (DEPTH, FFN_DIM, D), FFN_DIM ** -0.5),
    }


def reference(x, c, ada_w, ada_b, norm_mix, norm_ffn, sb_w_in, sb_w_out, dsa_w_in,
              dsa_q_norm, dsa_k_norm, dsa_w_uv, dsa_w_out, rel_bias, ffn_w_up,
              ffn_conv_w, ffn_conv_b, ffn_w_down):
    cond = jax.nn.silu(c)
    for i in range(DEPTH):
        mod = cond @ ada_w[i] + ada_b[i]
        sh1, sc1, g1, sh2, sc2, g2 = [m[:, None, :] for m in jnp.split(mod, 6, axis=-1)]
        h = rms_norm(x, norm_mix[i]) * (1.0 + sc1) + sh1
        j = i // N_MIXERS
        if i % N_MIXERS == 0:
            mix = stick_breaking_attention(h, sb_w_in[j], sb_w_out[j])
        else:
            mix = dsa_attention(h, dsa_w_in[j], dsa_q_norm[j], dsa_k_norm[j],
                                dsa_w_uv[j], dsa_w_out[j], rel_bias)
        x = x + g1 * mix
        h = rms_norm(x, norm_ffn[i]) * (1.0 + sc2) + sh2
        x = x + g2 * conv_ffn(h, ffn_w_up[i], ffn_conv_w[i], ffn_conv_b[i], ffn_w_down[i])
    return x
```

```python
import math
from contextlib import ExitStack

import numpy as np
import ml_dtypes

import concourse.bass as bass
import concourse.mybir as mybir
from concourse.bass_utils import run_bass_kernel_spmd

F32 = mybir.dt.float32
BF16 = mybir.dt.bfloat16
ALU = mybir.AluOpType
AF = mybir.ActivationFunctionType

S = 4096
D = 1024
NT = S // 128
FFN = 2816
NCH = 2 * FFN // 128
EPS = 1e-6
SB_BASE = 16512
SB_TOP = 229344

C_ID, C_TRI, C_NEGM, C_NEGU, C_ONES, C_NEGI, C_CMASK, C_IREP = 0, 128, 256, 384, 512, 640, 768, 896
NCONST = 896 + 512
IDX_LIM = 128.0
IDX_NIT = 24


class Region:
    __slots__ = ("w", "r")

    def __init__(self):
        self.w = None
        self.r = {}


class Prog:
    ENG = ("pe", "act", "dve", "pool", "sp")

    def __init__(self, nc):
        self.nc = nc
        self.streams = {e: [] for e in self.ENG}
        self.count = {}
        self.waited = {e: {} for e in self.ENG}
        self.cur = SB_BASE
        self.nalloc = 0
        self.nwaits = 0

    def alloc(self, shape, dtype, name="t"):
        nbytes = int(np.prod(shape[1:])) * (4 if dtype == F32 else 2)
        nbytes = (nbytes + 63) // 64 * 64
        off = self.cur
        assert off + nbytes <= SB_TOP, f"SBUF overflow allocating {name} {shape}"
        self.cur += nbytes
        self.nalloc += 1
        return self.nc.alloc_sbuf_tensor_at(f"{name}_{self.nalloc}", list(shape), dtype, offset=off)

    def mark(self):
        return self.cur

    def reset(self, m):
        self.cur = m

    def _need(self, eng, waits, tok):
        if tok is None:
            return
        k, v = tok
        if self.waited[eng].get(k, 0) >= v:
            return
        if waits.get(k, 0) < v:
            waits[k] = v

    def _deps(self, eng, reads, writes):
        waits = {}
        for r in reads:
            if r.w is not None:
                if not (eng == "pe" and r.w[0] == "pe"):
                    self._need(eng, waits, r.w)
        for w in writes:
            if w.w is not None and w.w[0] != eng:
                self._need(eng, waits, w.w)
            for k, v in w.r.items():
                if k != eng:
                    self._need(eng, waits, (k, v))
        for k, v in waits.items():
            self.waited[eng][k] = v
        return list(waits.items())

    def _commit(self, tok, reads, writes):
        k, v = tok
        for r in reads:
            if r.r.get(k, 0) < v:
                r.r[k] = v
        for w in writes:
            w.w = tok
            w.r = {}

    def op(self, eng, fn, reads=(), writes=()):
        waits = self._deps(eng, reads, writes)
        self.count[eng] = self.count.get(eng, 0) + 1
        tok = (eng, self.count[eng])
        self.streams[eng].append((waits, fn, eng, 1))
        self.nwaits += len(waits)
        self._commit(tok, reads, writes)

    def dma(self, q, out, in_, sem, reads=(), writes=()):
        waits = self._deps(q, reads, writes)
        self.count[sem] = self.count.get(sem, 0) + 16
        tok = (sem, self.count[sem])
        self.streams[q].append((waits, lambda e: e.dma_start(out=out, in_=in_), sem, 16))
        self.nwaits += len(waits)
        self._commit(tok, reads, writes)

    def barrier(self):
        snap = dict(self.count)
        for e in self.ENG:
            waits = []
            for k, v in snap.items():
                if v > 0 and self.waited[e].get(k, 0) < v:
                    waits.append((k, v))
                    self.waited[e][k] = v
            if waits:
                self.streams[e].append((waits, None, None, 0))

    def emit(self):
        nc = self.nc
        with ExitStack() as es:
            sems = {}
            for k in self.count:
                sems[k] = es.enter_context(nc.semaphore("s_" + k))
            block = es.enter_context(nc.Block())

            def run(stream):
                def f(e):
                    for waits, fn, semk, inc in stream:
                        for k, v in waits:
                            e.wait_ge(sems[k], v)
                        if fn is not None:
                            fn(e).then_inc(sems[semk], inc)
                return f

            block.tensor(run(self.streams["pe"]))
            block.scalar(run(self.streams["act"]))
            block.vector(run(self.streams["dve"]))
            block.gpsimd(run(self.streams["pool"]))
            block.sync(run(self.streams["sp"]))

    def mm(self, out, lhsT, rhs, start, stop, reads, writes):
        self.op("pe", lambda e: e.matmul(out, lhsT, rhs, start=start, stop=stop), reads, writes)

    def tr(self, out, in_, ident, reads, writes):
        self.op("pe", lambda e: e.transpose(out, in_, ident), reads, writes)

    def act(self, out, in_, func, reads, writes, bias=None, scale=None, accum_out=None):
        kw = {}
        if bias is not None:
            kw["bias"] = bias
        if scale is not None:
            kw["scale"] = scale
        if accum_out is not None:
            kw["accum_out"] = accum_out
        self.op("act", lambda e: e.activation(out=out, in_=in_, func=func, **kw), reads, writes)

    def ts(self, eng, out, in0, s1, s2, op0, op1, reads, writes, accum_out=None):
        kw = {}
        if accum_out is not None:
            kw["accum_out"] = accum_out
        if op1 is None:
            self.op(eng, lambda e: e.tensor_scalar(out=out, in0=in0, scalar1=s1, scalar2=None, op0=op0, **kw), reads, writes)
        else:
            self.op(eng, lambda e: e.tensor_scalar(out=out, in0=in0, scalar1=s1, scalar2=s2, op0=op0, op1=op1, **kw), reads, writes)

    def tt(self, eng, out, in0, in1, op, reads, writes):
        self.op(eng, lambda e: e.tensor_tensor(out=out, in0=in0, in1=in1, op=op), reads, writes)

    def stt(self, eng, out, in0, scalar, in1, op0, op1, reads, writes):
        self.op(eng, lambda e: e.scalar_tensor_tensor(out=out, in0=in0, scalar=scalar, in1=in1, op0=op0, op1=op1), reads, writes)

    def copy(self, eng, out, in_, reads, writes):
        self.op(eng, lambda e: e.tensor_copy(out=out, in_=in_), reads, writes)

    def memset(self, eng, ap, val, writes):
        self.op(eng, lambda e: e.memset(ap, val), (), writes)


class T:
    def __init__(self, t, n=1):
        self.t = t
        self.R = [Region() for _ in range(n)]
        self.r = self.R[0]


def build(n_layers=2, dbg=False, stop=None):
    nc = bass.Bass("TRN2", target_bir_lowering=False)
    P = Prog(nc)

    def din(name, shape, dt=F32):
        return nc.dram_tensor(name, list(shape), dt, kind="ExternalInput").ap()

    def dscr(name, shape, dt=BF16):
        kind = "ExternalOutput" if dbg else "Internal"
        return nc.dram_tensor(name, list(shape), dt, kind=kind).ap()

    x_in = din("x", [S, D])
    cT_in = din("cT", [128, 8])
    ada_w = din("ada_w", [2, D, 6 * D])
    ada_b = din("ada_b", [2, 6 * D])
    ada_bT = din("ada_bT", [128, 96])
    nmixT = din("nmixT", [128, 16])
    nffnT = din("nffnT", [128, 16])
    sb_w_in = din("sb_w_in", [D, 3 * D])
    sb_w_out = din("sb_w_out", [D, D])
    dsa_w_out_in = din("dsa_w_out", [D, D])
    ffn_w_up = din("ffn_w_up", [2, D, 2 * FFN])
    ffn_w_down = din("ffn_w_down", [2, FFN, D])
    convT = din("convT", [128, 2 * NCH * 4])
    consts_in = din("consts", [128, NCONST], BF16)
    dsa_w_in = din("dsa_w_in", [D, 2760])
    dsa_gT = din("dsa_gT", [128, 2])
    dsa_w_uv = din("dsa_w_uv", [16, 128, 64])
    biasg_in = din("biasg", [128, 4096])
    biasc_in = din("biasc", [128, 4096])
    y = nc.dram_tensor("y", [S, D], F32, kind="ExternalOutput").ap()

    qkT_s = dscr("qkT_s", [16, 128, S])
    v_s = dscr("v_s", [S, D])
    oT_s = dscr("oT_s", [D, S])
    qn_s = dscr("qn_s", [NT, 128, 2048])
    wo_s = dscr("wo_s", [2, 128, 8, D])
    wd_s = dscr("wd_s", [2, 128, 22, D])

    Ry = [Region() for _ in range(NT)]
    Rqk = [[Region() for _ in range(8)] for _ in range(16)]
    Rv = [Region() for _ in range(NT)]
    Ro = [[Region() for _ in range(8)] for _ in range(16)]
    Rqn = [Region() for _ in range(NT)]
    Rwo = [Region(), Region()]
    Rwd = [Region(), Region()]

    PS = [T(nc.alloc_psum_tensor(f"ps{i}", [128, 512], F32)) for i in range(6)]
    PT = [T(nc.alloc_psum_tensor(f"pt{i}", [128, 1024], BF16)) for i in range(2)]

    consts = T(P.alloc([128, NCONST], BF16, "consts"))
    cst = consts.t
    ident = cst[:, C_ID:C_ID + 128]
    tri = cst[:, C_TRI:C_TRI + 128]
    negm = cst[:, C_NEGM:C_NEGM + 128]
    negU = cst[:, C_NEGU:C_NEGU + 128]
    ones = cst[:, C_ONES:C_ONES + 128]
    negI = cst[:, C_NEGI:C_NEGI + 128]
    cmask = cst[:, C_CMASK:C_CMASK + 128]
    irep4 = cst[:, C_IREP:C_IREP + 512]
    modfm = T(P.alloc([128, 96], F32, "modfm"))
    gs = T(P.alloc([128, 32], F32, "gs"))
    convw = T(P.alloc([128, 2 * NCH * 4], F32, "convw"))
    P.dma("sp", consts.t[:], consts_in[:, :], "ld_c0", (), [consts.r])
    P.dma("sp", convw.t[:], convT[:, :], "ld_c1", (), [convw.r])
    pers_mark = P.mark()

    def sh_ap(layer, which, c):
        col = layer * 48 + (0 if which == 0 else 24) + c
        return modfm.t[:, col:col + 1]

    def gs_ap(layer, which, c):
        col = layer * 16 + which * 8 + c
        return gs.t[:, col:col + 1]

    def phase_P():
        cT = T(P.alloc([128, 8], F32, "cT"))
        cond = T(P.alloc([128, 8], F32, "cond"))
        condb = T(P.alloc([128, 8], BF16, "condb"))
        crep = T(P.alloc([128, 8, 128], BF16, "crep"))
        abT = T(P.alloc([128, 96], F32, "abT"))
        nmx = T(P.alloc([128, 16], F32, "nmx"))
        nff = T(P.alloc([128, 16], F32, "nff"))
        P.dma("sp", cT.t[:], cT_in[:, :], "ld_p0", (), [cT.r])
        P.dma("sp", abT.t[:], ada_bT[:, :], "ld_p1", (), [abT.r])
        P.dma("sp", nmx.t[:], nmixT[:, :], "ld_p2", (), [nmx.r])
        P.dma("sp", nff.t[:], nffnT[:, :], "ld_p3", (), [nff.r])
        P.act(cond.t[:], cT.t[:], AF.Silu, [cT.r], [cond.r])
        P.copy("dve", condb.t[:], cond.t[:], [cond.r], [condb.r])
        for kc in range(8):
            P.ts("dve", crep.t[:, kc, :], ones, cond.t[:, kc:kc + 1], None, ALU.mult, None,
                 [consts.r, cond.r], [crep.r])
        wp = [T(P.alloc([128, 8, 1024], BF16, "adawp")) for _ in range(2)]
        abb = [T(P.alloc([128, 1024], F32, "abb")) for _ in range(2)]
        gbc = [T(P.alloc([128, 1024], F32, "gbc")) for _ in range(2)]
        wst = [T(P.alloc([128, 1024], F32, "wst")) for _ in range(3)]
        wob = [T(P.alloc([128, 1024], BF16, "wob")) for _ in range(3)]
        PM = PS[0]
        npiece = 0
        nst = 0
        for layer in range(n_layers):
            for g in range(6):
                w = wp[npiece % 2]
                for kc in range(8):
                    P.dma("pool", w.t[:, kc, :], ada_w[layer, kc * 128:(kc + 1) * 128, g * 1024:(g + 1) * 1024],
                          f"ld_aw{npiece % 2}", (), [w.r])
                npiece += 1
                if g in (0, 1, 3, 4):
                    for oc in range(8):
                        col = layer * 48 + g * 8 + oc
                        for kc in range(8):
                            P.mm(PM.t[:, col:col + 1], w.t[:, kc, oc * 128:(oc + 1) * 128], condb.t[:, kc:kc + 1],
                                 kc == 0, kc == 7, [w.r, condb.r], [PM.r])
                else:
                    which = 0 if g == 2 else 1
                    ab = abb[which]
                    P.dma("sp", ab.t[:], ada_b[layer:layer + 1, g * 1024:(g + 1) * 1024].partition_broadcast(128),
                          f"ld_abb{which}", (), [ab.r])
                    gb = gbc[which]
                    for half in range(2):
                        PG = PS[1 + half]
                        for kc in range(8):
                            P.mm(PG.t[:, :], crep.t[:, kc, :], w.t[:, kc, half * 512:(half + 1) * 512],
                                 kc == 0, kc == 7, [w.r, crep.r], [PG.r])
                        P.tt("dve", gb.t[:, half * 512:(half + 1) * 512], PG.t[:, :], ab.t[:, half * 512:(half + 1) * 512],
                             ALU.add, [PG.r, ab.r], [gb.r])
                    if which == 0:
                        srcs = [(sb_w_out if layer == 0 else dsa_w_out_in)[c * 128:(c + 1) * 128, :] for c in range(8)]
                        dsts = [wo_s[layer, :, c, :] for c in range(8)]
                        Rdst = Rwo[layer]
                    else:
                        srcs = [ffn_w_down[layer, c * 128:(c + 1) * 128, :] for c in range(22)]
                        dsts = [wd_s[layer, :, c, :] for c in range(22)]
                        Rdst = Rwd[layer]
                    for sa, da in zip(srcs, dsts):
                        ws = wst[nst % 3]
                        wb = wob[nst % 3]
                        P.dma("sp", ws.t[:], sa, f"ld_wst{nst % 3}", (), [ws.r])
                        P.tt("dve" if nst % 2 == 0 else "pool", wb.t[:], ws.t[:], gb.t[:], ALU.mult, [ws.r, gb.r], [wb.r])
                        P.dma("sp", da, wb.t[:], f"st_wob{nst % 3}", [wb.r], [Rdst])
                        nst += 1
        ncol = 48 * n_layers
        P.tt("dve", modfm.t[:, 0:ncol], PM.t[:, 0:ncol], abT.t[:, 0:ncol], ALU.add, [PM.r, abT.r], [modfm.r])
        for layer in range(n_layers):
            for which in range(2):
                sc0 = layer * 48 + (8 if which == 0 else 32)
                nrm = (nmx if which == 0 else nff).t[:, layer * 8:(layer + 1) * 8]
                P.stt("dve", gs.t[:, layer * 16 + which * 8: layer * 16 + which * 8 + 8],
                      modfm.t[:, sc0:sc0 + 8], 1.0, nrm, ALU.add, ALU.mult, [modfm.r, nmx.r, nff.r], [gs.r])


    class NormBufs:
        def __init__(self, ntt):
            self.ntt = ntt
            self.junk = T(P.alloc([128, 1024], BF16, "junk"))
            self.ss = T(P.alloc([128, 4], F32, "ss"))
            self.rs = T(P.alloc([128, 4], F32, "rs"))
            self.rstd = T(P.alloc([128, 4], F32, "rstd"))
            self.xn = T(P.alloc([128, ntt, 1024], BF16, "xn"))
            self.n = 0

    def emit_norm_T(nb, xs, hT, hoff, layer, which):
        ntt = nb.ntt
        P.memset("pool", nb.ss.t[:], 0.0, [nb.ss.r])
        for j in range(ntt):
            P.act(nb.junk.t[:], xs.t[:, j, :], AF.Square, [xs.r], [nb.junk.r, nb.ss.r], accum_out=nb.ss.t[:, j:j + 1])
        P.act(nb.rs.t[:, 0:ntt], nb.ss.t[:, 0:ntt], AF.Sqrt, [nb.ss.r], [nb.rs.r], bias=EPS, scale=1.0 / D)
        P.op("dve", lambda e: e.reciprocal(out=nb.rstd.t[:, 0:ntt], in_=nb.rs.t[:, 0:ntt]), [nb.rs.r], [nb.rstd.r])
        for j in range(ntt):
            P.ts("dve" if j % 2 == 0 else "pool", nb.xn.t[:, j, :], xs.t[:, j, :], nb.rstd.t[:, j:j + 1], None,
                 ALU.mult, None, [xs.r, nb.rstd.r], [nb.xn.r])
        W = ntt * 128
        per = 1024 // W
        c = 0
        while c < 8:
            pt = PT[nb.n % 2]
            nb.n += 1
            for cc in range(per):
                for j in range(ntt):
                    P.tr(pt.t[:, cc * W + j * 128: cc * W + (j + 1) * 128], nb.xn.t[:, j, (c + cc) * 128:(c + cc + 1) * 128],
                         ident, [nb.xn.r, consts.r], [pt.r])
            for cc in range(per):
                ch = c + cc
                if ch % 2 == 0:
                    P.act(hT.t[:, ch, hoff:hoff + W], pt.t[:, cc * W:(cc + 1) * W], AF.Identity, [pt.r, gs.r, modfm.r], [hT.r],
                          bias=sh_ap(layer, which, ch), scale=gs_ap(layer, which, ch))
                else:
                    P.ts("dve", hT.t[:, ch, hoff:hoff + W], pt.t[:, cc * W:(cc + 1) * W], gs_ap(layer, which, ch),
                         sh_ap(layer, which, ch), ALU.mult, ALU.add, [pt.r, gs.r, modfm.r], [hT.r])
            c += per

    def x_src(layer):
        return x_in if layer == 0 else y

    def phase_A0():
        m = P.mark()
        win = T(P.alloc([128, 8, 3 * D], BF16, "win"))
        for kc in range(8):
            P.dma("pool", win.t[:, kc, :], sb_w_in[kc * 128:(kc + 1) * 128, :], "ld_win", (), [win.r])
        xs = [T(P.alloc([128, 4, 1024], F32, "xs")) for _ in range(2)]
        nb = NormBufs(4)
        hT = [T(P.alloc([128, 8, 512], BF16, "hT")) for _ in range(2)]
        qk = [T(P.alloc([128, 512], BF16, "qk")) for _ in range(4)]
        vs = [T(P.alloc([128, 1024], BF16, "vs")) for _ in range(2)]

        def load(Tq):
            xb = xs[Tq % 2]
            P.dma("sp", xb.t[:], x_in[Tq * 512:(Tq + 1) * 512, :].rearrange("(j p) d -> p j d", p=128),
                  f"ld_xs{Tq % 2}", [], [xb.r])

        load(0)
        nev = 0
        for Tq in range(8):
            if Tq + 1 < 8:
                load(Tq + 1)
            xb = xs[Tq % 2]
            h = hT[Tq % 2]
            emit_norm_T(nb, xb, h, 0, 0, 0)
            for oc in range(16):
                ps = PS[oc % 4]
                for kc in range(8):
                    P.mm(ps.t[:, :], win.t[:, kc, oc * 128:(oc + 1) * 128], h.t[:, kc, :], kc == 0, kc == 7,
                         [win.r, h.r], [ps.r])
                qb = qk[oc % 4]
                sc = 0.125 if oc < 8 else 1.0
                if nev % 2 == 0:
                    P.act(qb.t[:], ps.t[:, :], AF.Identity, [ps.r], [qb.r], scale=sc)
                else:
                    P.ts("dve", qb.t[:], ps.t[:, :], sc, None, ALU.mult, None, [ps.r], [qb.r])
                nev += 1
                P.dma("sp", qkT_s[oc, :, Tq * 512:(Tq + 1) * 512], qb.t[:], f"st_qk{oc % 4}", [qb.r],
                      [Rqk[oc][Tq]])
            for j in range(4):
                vb = vs[j % 2]
                for half in range(2):
                    ps = PS[4 + half]
                    for kc in range(8):
                        P.mm(ps.t[:, :], h.t[:, kc, j * 128:(j + 1) * 128], win.t[:, kc, 2048 + half * 512: 2048 + (half + 1) * 512],
                             kc == 0, kc == 7, [win.r, h.r], [ps.r])
                    if nev % 2 == 0:
                        P.act(vb.t[:, half * 512:(half + 1) * 512], ps.t[:, :], AF.Identity, [ps.r], [vb.r])
                    else:
                        P.copy("dve", vb.t[:, half * 512:(half + 1) * 512], ps.t[:, :], [ps.r], [vb.r])
                    nev += 1
                tile = Tq * 4 + j
                P.dma("sp", v_s[tile * 128:(tile + 1) * 128, :], vb.t[:], f"st_v{j % 2}", [vb.r], [Rv[tile]])
        P.barrier()
        P.reset(m)

    def phase_B0(heads=range(16), Qs=range(8)):
        m = P.mark()
        kT = [T(P.alloc([64, S], BF16, "kT")) for _ in range(2)]
        vh = [T(P.alloc([128, NT, 64], BF16, "vh")) for _ in range(2)]
        qT = [T(P.alloc([64, 512], BF16, "qT")) for _ in range(2)]
        u = [T(P.alloc([128, 512], F32, "u")) for _ in range(2)]
        sp = [T(P.alloc([128, 512], BF16, "sp")) for _ in range(2)]
        tot = [T(P.alloc([128, 512], BF16, "tot")) for _ in range(2)]
        A = [T(P.alloc([128, 512], BF16, "A")) for _ in range(2)]
        osb = [T(P.alloc([64, 512], BF16, "osb")) for _ in range(2)]
        PZ = [PS[0], PS[1]]
        PC = [PS[2], PS[3]]
        PL = PS[4]
        PO = PS[5]
        work = [(h, Q) for h in heads for Q in Qs]

        def load_head(hi, h):
            kb_ = kT[hi % 2]
            vb_ = vh[hi % 2]
            oc = 8 + h // 2
            r0 = (h % 2) * 64
            P.dma("sp", kb_.t[:], qkT_s[oc, r0:r0 + 64, :], f"ld_kT{hi % 2}", [Rqk[oc][t_] for t_ in range(8)], [kb_.r])
            for q4 in range(4):
                P.dma("sp", vb_.t[:, q4 * 8:(q4 + 1) * 8, :],
                      v_s[q4 * 1024:(q4 + 1) * 1024, h * 64:(h + 1) * 64].rearrange("(k p) d -> p k d", p=128),
                      f"ld_vh{hi % 2}", Rv, [vb_.r])

        def load_q(wi):
            h, Q = work[wi]
            qb = qT[wi % 2]
            oc = h // 2
            r0 = (h % 2) * 64
            P.dma("sp", qb.t[:], qkT_s[oc, r0:r0 + 64, Q * 512:(Q + 1) * 512], f"ld_qT{wi % 2}", [Rqk[oc][Q]], [qb.r])

        hlist = list(heads)
        load_head(0, hlist[0])
        load_q(0)
        gi = [0]
        for wi, (h, Q) in enumerate(work):
            hi = hlist.index(h)
            if Q == list(Qs)[0] and hi + 1 < len(hlist):
                load_head(hi + 1, hlist[hi + 1])
            if wi + 1 < len(work):
                load_q(wi + 1)
            kb_ = kT[hi % 2]
            vb_ = vh[hi % 2]
            qb = qT[wi % 2]
            kbs = list(range(4 * Q + 3, -1, -1))
            n = len(kbs)
            base = gi[0]
            gi[0] += n

            def s1(i):
                kb = kbs[i]
                j = kb - 4 * Q
                c0 = max(j, 0) * 128
                b = (base + i) % 2
                P.mm(PZ[b].t[:, c0:512], kb_.t[:, kb * 128:(kb + 1) * 128], qb.t[:, c0:512], True, True,
                     [kb_.r, qb.r], [PZ[b].r])
                P.act(u[b].t[:, c0:512], PZ[b].t[:, c0:512], AF.Exp, [PZ[b].r], [u[b].r])
                P.act(sp[b].t[:, c0:512], u[b].t[:, c0:512], AF.Ln, [u[b].r], [sp[b].r], bias=1.0)
                if j >= 0:
                    P.tt("dve", sp[b].t[:, c0:c0 + 128], sp[b].t[:, c0:c0 + 128], tri, ALU.mult, [sp[b].r, consts.r], [sp[b].r])

            def s2(i):
                kb = kbs[i]
                j = kb - 4 * Q
                c0 = max(j, 0) * 128
                b = (base + i) % 2
                pb = (base + i - 1) % 2
                c1 = c0 + 128 if j >= 0 else c0
                pc = PC[b]
                has_tot = i > 0 and c1 < 512
                P.mm(pc.t[:, c0:512], kb_.t[:, kb * 128:(kb + 1) * 128], qb.t[:, c0:512], True, False,
                     [kb_.r, qb.r], [pc.r])
                P.mm(pc.t[:, c0:512], negU, sp[b].t[:, c0:512], False, not (has_tot or j >= 0), [consts.r, sp[b].r], [pc.r])
                if has_tot:
                    P.mm(pc.t[:, c1:512], negI, tot[pb].t[:, c1:512], False, not (j >= 0), [consts.r, tot[pb].r], [pc.r])
                if j >= 0:
                    P.mm(pc.t[:, c0:c0 + 128], ident, negm, False, True, [consts.r], [pc.r])
                P.act(A[b].t[:, c0:512], pc.t[:, c0:512], AF.Exp, [pc.r], [A[b].r])
                if j >= 0:
                    P.mm(PO.t[0:64, c0:c0 + 128], vb_.t[:, kb, :], A[b].t[:, c0:c0 + 128], i == 0, False, [vb_.r, A[b].r], [PO.r])
                    if c1 < 512:
                        P.mm(PO.t[0:64, c1:512], vb_.t[:, kb, :], A[b].t[:, c1:512], False, kb == 0, [vb_.r, A[b].r], [PO.r])
                else:
                    P.mm(PO.t[0:64, :], vb_.t[:, kb, :], A[b].t[:, :], False, kb == 0, [vb_.r, A[b].r], [PO.r])
                if i < n - 1:
                    if j >= 0:
                        P.mm(PL.t[:, c0:c0 + 128], ones, sp[b].t[:, c0:c0 + 128], i == 0, False, [consts.r, sp[b].r], [PL.r])
                        if c1 < 512:
                            P.mm(PL.t[:, c1:512], ones, sp[b].t[:, c1:512], False, False, [consts.r, sp[b].r], [PL.r])
                    else:
                        P.mm(PL.t[:, :], ones, sp[b].t[:, :], False, False, [consts.r, sp[b].r], [PL.r])
                    P.copy("dve", tot[b].t[:, c0:512], PL.t[:, c0:512], [PL.r], [tot[b].r])

            s1(0)
            for i in range(n):
                if i + 1 < n:
                    s1(i + 1)
                s2(i)
            ob = osb[wi % 2]
            P.copy("dve", ob.t[:], PO.t[0:64, :], [PO.r], [ob.r])
            P.dma("sp", oT_s[h * 64:(h + 1) * 64, Q * 512:(Q + 1) * 512], ob.t[:], f"st_o{wi % 2}", [ob.r], [Ro[h][Q]])
        P.barrier()
        P.reset(m)

    def phase_C(layer):
        m = P.mark()
        wo = T(P.alloc([128, 8, D], BF16, "wo"))
        P.dma("sp", wo.t[:], wo_s[layer], "ld_wo", [Rwo[layer]], [wo.r])
        ob = [T(P.alloc([128, 8, 512], BF16, "ob")) for _ in range(2)]
        xs = [T(P.alloc([128, 4, 1024], F32, "xs")) for _ in range(2)]
        src = x_src(layer)

        def load(Tq):
            o_ = ob[Tq % 2]
            x_ = xs[Tq % 2]
            P.dma("sp", o_.t[:], oT_s[:, Tq * 512:(Tq + 1) * 512].rearrange("(c p) t -> p c t", p=128), f"ld_ob{Tq % 2}",
                  [Ro[h][Tq] for h in range(16)], [o_.r])
            P.dma("sp", x_.t[:], src[Tq * 512:(Tq + 1) * 512, :].rearrange("(j p) d -> p j d", p=128), f"ld_xs{Tq % 2}",
                  [Ry[Tq * 4 + j] for j in range(4)] if layer > 0 else [], [x_.r])

        load(0)
        nps = 0
        for Tq in range(8):
            if Tq + 1 < 8:
                load(Tq + 1)
            o_ = ob[Tq % 2]
            x_ = xs[Tq % 2]
            for j in range(4):
                for half in range(2):
                    ps = PS[nps % 4]
                    nps += 1
                    for kc in range(8):
                        P.mm(ps.t[:, :], o_.t[:, kc, j * 128:(j + 1) * 128], wo.t[:, kc, half * 512:(half + 1) * 512],
                             kc == 0, kc == 7, [o_.r, wo.r], [ps.r])
                    P.tt("dve", x_.t[:, j, half * 512:(half + 1) * 512], ps.t[:, :], x_.t[:, j, half * 512:(half + 1) * 512],
                         ALU.add, [ps.r, x_.r], [x_.r])
            P.dma("sp", y[Tq * 512:(Tq + 1) * 512, :].rearrange("(j p) d -> p j d", p=128), x_.t[:], f"st_xs{Tq % 2}",
                  [x_.r], [Ry[Tq * 4 + j] for j in range(4)])
        P.barrier()
        P.reset(m)

    def phase_F(layer):
        m = P.mark()
        TT = 2
        W = TT * 128
        NS = S // W
        wup = T(P.alloc([128, 8, 2 * FFN], BF16, "wup"))
        for kc in range(8):
            P.dma("pool", wup.t[:, kc, :], ffn_w_up[layer, kc * 128:(kc + 1) * 128, :], "ld_wup", (), [wup.r])
        wd = T(P.alloc([128, 22, D], BF16, "wd"))
        P.dma("sp", wd.t[:], wd_s[layer], "ld_wd", [Rwd[layer]], [wd.r])
        xs = [T(P.alloc([128, TT, 1024], F32, "xs")) for _ in range(2)]
        nb = NormBufs(TT)
        hT = [T(P.alloc([128, 8, W + 2], BF16, "hT")) for _ in range(2)]
        gv = T(P.alloc([128, 22, W], BF16, "gv"))
        ya = [T(P.alloc([128, W], F32, "ya")) for _ in range(4)]
        yb = [T(P.alloc([128, W], F32, "yb")) for _ in range(4)]
        sg = [T(P.alloc([128, W], F32, "sg")) for _ in range(2)]

        def cw(ch, k):
            col = (layer * NCH + ch) * 4 + k
            return convw.t[:, col:col + 1]

        def load(Ti):
            x_ = xs[Ti % 2]
            P.dma("sp", x_.t[:], y[Ti * W:(Ti + 1) * W, :].rearrange("(j p) d -> p j d", p=128), f"ld_xs{Ti % 2}",
                  [Ry[Ti * TT + j] for j in range(TT)], [x_.r])

        load(0)
        P.memset("pool", hT[0].t[:, :, 0:2], 0.0, [hT[0].r])
        nu = 0
        nd = 0
        for Ti in range(NS):
            if Ti + 1 < NS:
                load(Ti + 1)
            x_ = xs[Ti % 2]
            h = hT[Ti % 2]
            if Ti > 0:
                hp = hT[(Ti - 1) % 2]
                P.copy("pool", h.t[:, :, 0:2], hp.t[:, :, W:W + 2], [hp.r], [h.r])
            emit_norm_T(nb, x_, h, 2, layer, 1)
            for mch in range(22):
                outs = []
                for side in range(2):
                    ch = mch + 22 * side
                    ps = PS[nu % 4]
                    a = ya[nu % 4]
                    b = yb[nu % 4]
                    nu += 1
                    for kc in range(8):
                        P.mm(ps.t[:, 0:W + 2], wup.t[:, kc, ch * 128:(ch + 1) * 128], h.t[:, kc, :], kc == 0, kc == 7,
                             [wup.r, h.r], [ps.r])
                    P.act(a.t[:], ps.t[:, 2:W + 2], AF.Identity, [ps.r, convw.r], [a.r], bias=cw(ch, 3), scale=cw(ch, 2))
                    P.stt("dve", b.t[:], ps.t[:, 1:W + 1], cw(ch, 1), a.t[:], ALU.mult, ALU.add, [ps.r, a.r, convw.r], [b.r])
                    P.stt("dve", a.t[:], ps.t[:, 0:W], cw(ch, 0), b.t[:], ALU.mult, ALU.add, [ps.r, b.r, convw.r], [a.r])
                    outs.append(a)
                s_ = sg[mch % 2]
                P.act(s_.t[:], outs[0].t[:], AF.Silu, [outs[0].r], [s_.r])
                P.tt("pool", gv.t[:, mch, :], s_.t[:], outs[1].t[:], ALU.mult, [s_.r, outs[1].r], [gv.r])
            for j in range(TT):
                for half in range(2):
                    ps = PS[4 + nd % 2]
                    nd += 1
                    for mch in range(22):
                        P.mm(ps.t[:, :], gv.t[:, mch, j * 128:(j + 1) * 128], wd.t[:, mch, half * 512:(half + 1) * 512],
                             mch == 0, mch == 21, [gv.r, wd.r], [ps.r])
                    P.tt("dve", x_.t[:, j, half * 512:(half + 1) * 512], ps.t[:, :], x_.t[:, j, half * 512:(half + 1) * 512],
                         ALU.add, [ps.r, x_.r], [x_.r])
            P.dma("sp", y[Ti * W:(Ti + 1) * W, :].rearrange("(j p) d -> p j d", p=128), x_.t[:], f"st_xs{Ti % 2}",
                  [x_.r], [Ry[Ti * TT + j] for j in range(TT)])
        P.barrier()
        P.reset(m)


    class L1:
        pass

    def alloc_L1():
        L = L1()
        L.knT = T(P.alloc([128, S], BF16, "knT"))
        L.vals = T(P.alloc([128, NT, 129], BF16, "vals"))
        L.kiT2 = T(P.alloc([128, S], BF16, "kiT2"))
        L.qiT = T(P.alloc([128, 4, S], BF16, "qiT"))
        L.wi = T(P.alloc([128, NT, 8], F32, "wi"))
        L.gq = T(P.alloc([128, 2], F32, "gq"))
        return L

    def phase_A1(L):
        m = P.mark()
        win = T(P.alloc([128, 8, 2760], BF16, "win1"))
        for kc in range(8):
            P.dma("pool", win.t[:, kc, :], dsa_w_in[kc * 128:(kc + 1) * 128, :], "ld_win", (), [win.r])
        wki2 = T(P.alloc([128, 8, 128], BF16, "wki2"))
        P.copy("dve", wki2.t[:, :, 0:64], win.t[:, :, 2688:2752], [win.r], [wki2.r])
        P.copy("dve", wki2.t[:, :, 64:128], win.t[:, :, 2688:2752], [win.r], [wki2.r])
        gT = T(P.alloc([128, 2], F32, "gT"))
        P.dma("sp", gT.t[:], dsa_gT[:, :], "ld_gT", (), [gT.r])
        P.ts("dve", L.gq.t[:, 0:1], gT.t[:, 0:1], 128.0 ** -0.5, None, ALU.mult, None, [gT.r], [L.gq.r])
        P.copy("dve", L.gq.t[:, 1:2], gT.t[:, 1:2], [gT.r], [L.gq.r])
        P.memset("pool", L.vals.t[:], 1.0, [L.vals.r])
        xs = [T(P.alloc([128, 4, 1024], F32, "xs")) for _ in range(2)]
        nb = NormBufs(4)
        hT = [T(P.alloc([128, 8, 512], BF16, "hT")) for _ in range(2)]
        qnb = T(P.alloc([128, 4, 16, 128], BF16, "qnb"))
        SQ = [T(P.alloc([128, 512], BF16, "sq")) for _ in range(2)]
        RS = [T(P.alloc([128, 512], F32, "rs")) for _ in range(2)]

        def load(Tq):
            xb = xs[Tq % 2]
            P.dma("sp", xb.t[:], y[Tq * 512:(Tq + 1) * 512, :].rearrange("(j p) d -> p j d", p=128),
                  f"ld_xs{Tq % 2}", [Ry[Tq * 4 + j] for j in range(4)], [xb.r])

        load(0)
        n = 0
        nev = 0
        for Tq in range(8):
            if Tq + 1 < 8:
                load(Tq + 1)
            xb = xs[Tq % 2]
            h = hT[Tq % 2]
            emit_norm_T(nb, xb, h, 0, 1, 0)
            for hd in range(17):
                col0 = hd * 128 if hd < 16 else 2048
                psq = PS[n % 2]
                pss = PS[2 + n % 2]
                sq = SQ[n % 2]
                rs = RS[n % 2]
                n += 1
                for kc in range(8):
                    P.mm(psq.t[:, :], win.t[:, kc, col0:col0 + 128], h.t[:, kc, :], kc == 0, kc == 7, [win.r, h.r], [psq.r])
                P.act(sq.t[:], psq.t[:, :], AF.Square, [psq.r], [sq.r])
                P.mm(pss.t[:, :], ones, sq.t[:], True, True, [consts.r, sq.r], [pss.r])
                P.act(rs.t[:], pss.t[:, :], AF.Sqrt, [pss.r], [rs.r], bias=EPS, scale=1.0 / 128)
                P.op("dve", lambda e, rs=rs: e.reciprocal(out=rs.t[:], in_=rs.t[:]), [rs.r], [rs.r])
                if hd < 16:
                    P.stt("dve", qnb.t[:, :, hd, :], psq.t[:, :].rearrange("p (j t) -> p j t", j=4), L.gq.t[:, 0:1],
                          rs.t[:].rearrange("p (j t) -> p j t", j=4), ALU.mult, ALU.mult, [psq.r, rs.r, L.gq.r], [qnb.r])
                else:
                    P.stt("dve", L.knT.t[:, Tq * 512:(Tq + 1) * 512], psq.t[:, :], L.gq.t[:, 1:2], rs.t[:], ALU.mult, ALU.mult,
                          [psq.r, rs.r, L.gq.r], [L.knT.r])
            for j in range(4):
                P.dma("sp", qn_s[Tq * 4 + j], qnb.t[:, j, :, :].rearrange("p h t -> p (h t)"), "st_qn", [qnb.r], [Rqn[Tq * 4 + j]])

            def evac(out, in_, reads, writes, scale=None):
                nonlocal nev
                if nev % 2 == 0:
                    if scale is None:
                        P.act(out, in_, AF.Identity, reads, writes)
                    else:
                        P.act(out, in_, AF.Identity, reads, writes, scale=scale)
                else:
                    if scale is None:
                        P.copy("dve", out, in_, reads, writes)
                    else:
                        P.ts("dve", out, in_, scale, None, ALU.mult, None, reads, writes)
                nev += 1

            for j in range(4):
                ps = PS[4]
                for kc in range(8):
                    P.mm(ps.t[:, 0:128], h.t[:, kc, j * 128:(j + 1) * 128], win.t[:, kc, 2048:2176], kc == 0, kc == 7,
                         [win.r, h.r], [ps.r])
                evac(L.vals.t[:, Tq * 4 + j, 0:128], ps.t[:, 0:128], [ps.r], [L.vals.r])
            for c in range(4):
                ps = PS[5]
                for kc in range(8):
                    P.mm(ps.t[:, :], win.t[:, kc, 2176 + c * 128: 2176 + (c + 1) * 128], h.t[:, kc, :], kc == 0, kc == 7,
                         [win.r, h.r], [ps.r])
                evac(L.qiT.t[:, c, Tq * 512:(Tq + 1) * 512], ps.t[:, :], [ps.r], [L.qiT.r])
            ps = PS[4]
            for kc in range(8):
                P.mm(ps.t[:, :], wki2.t[:, kc, :], h.t[:, kc, :], kc == 0, kc == 7, [wki2.r, h.r], [ps.r])
            evac(L.kiT2.t[:, Tq * 512:(Tq + 1) * 512], ps.t[:, :], [ps.r], [L.kiT2.r])
            for j in range(4):
                ps = PS[5]
                for kc in range(8):
                    P.mm(ps.t[:, 0:8], h.t[:, kc, j * 128:(j + 1) * 128], win.t[:, kc, 2752:2760], kc == 0, kc == 7,
                         [win.r, h.r], [ps.r])
                evac(L.wi.t[:, Tq * 4 + j, :], ps.t[:, 0:8], [ps.r], [L.wi.r], scale=8.0 ** -0.5)
        P.barrier()
        P.reset(m)

    def phase_B1(L, qts=range(NT)):
        m = P.mark()
        isc = [T(P.alloc([128, S], F32, "isc")) for _ in range(2)]
        pen = [T(P.alloc([128, S], BF16, "pen")) for _ in range(2)]
        dg = [T(P.alloc([128, 8, 128], BF16, "dg")) for _ in range(2)]
        Rh = [T(P.alloc([128, 512], BF16, "Rh")) for _ in range(8)]
        junk = T(P.alloc([128, S], BF16, "junkc"))
        thr = [T(P.alloc([128, 1], F32, "thr")) for _ in range(2)]
        cand = T(P.alloc([128, 1], F32, "cand"))
        ind = T(P.alloc([128, 1], F32, "ind"))
        cnt = [T(P.alloc([128, IDX_NIT], F32, "cnt")) for _ in range(2)]
        qnq = [T(P.alloc([128, 2048], BF16, "qnq")) for _ in range(2)]
        PTb = [T(P.alloc([128, 512], BF16, "PTb")) for _ in range(3)]
        olat = T(P.alloc([128, 2048], BF16, "olat"))
        olT = T(P.alloc([128, 2048], BF16, "olT"))
        oTq = T(P.alloc([128, 1024], BF16, "oTq"))
        rden = T(P.alloc([128, 16], F32, "rden"))
        xq = [T(P.alloc([128, 1024], F32, "xq")) for _ in range(2)]
        wo = T(P.alloc([128, 8, D], BF16, "wo1"))
        wuvP = T(P.alloc([128, 16, 128], BF16, "wuvP"))
        biasT = T(P.alloc([128, 2, 2048], BF16, "biasT"))
        P.dma("sp", wo.t[:], wo_s[1], "ld_wo", [Rwo[1]], [wo.r])
        P.memset("pool", wuvP.t[:], 0.0, [wuvP.r])
        for hd in range(16):
            c0 = (hd % 2) * 64
            P.dma("pool", wuvP.t[:, hd, c0:c0 + 64], dsa_w_uv[hd], "ld_wuv", (), [wuvP.r])
        P.dma("sp", isc[0].t[:], biasg_in[:, :], "ld_bg", (), [isc[0].r])
        P.dma("sp", isc[1].t[:], biasc_in[:, :], "ld_bc", (), [isc[1].r])
        P.tt("dve", biasT.t[:].rearrange("p w c -> p (w c)"), isc[0].t[:], isc[1].t[:], ALU.subtract, [isc[0].r, isc[1].r], [biasT.r])

        qtl = list(qts)
        ctr = {"z": 0, "ev": 0, "pt": 0}

        def ev2(out, in_, reads, writes, relu=False):
            k = ctr["ev"]
            ctr["ev"] += 1
            if k % 2 == 0:
                P.act(out, in_, AF.Relu if relu else AF.Identity, reads, writes)
            else:
                if relu:
                    P.ts("dve", out, in_, 0.0, None, ALU.max, None, reads, writes)
                else:
                    P.copy("dve", out, in_, reads, writes)

        def idx(qi_):
            qt = qtl[qi_]
            b = qi_ % 2
            nk = (qt + 1) * 128
            P.dma("sp", qnq[b].t[:], qn_s[qt], f"ld_qnq{b}", [Rqn[qt]], [qnq[b].r])
            P.dma("sp", xq[b].t[:], y[qt * 128:(qt + 1) * 128, :], f"ld_xq{b}", [Ry[qt]], [xq[b].r])
            for ih in range(8):
                P.ts("pool" if ih % 2 else "dve", dg[b].t[:, ih, :], ident, L.wi.t[:, qt, ih:ih + 1], None, ALU.mult, None,
                     [consts.r, L.wi.r], [dg[b].r])
            for ck in range((nk + 511) // 512):
                c_lo = ck * 512
                w = min(512, nk - c_lo)
                for ih in range(8):
                    pz = PS[ctr["z"] % 2]
                    ctr["z"] += 1
                    r0 = (ih % 2) * 64
                    P.mm(pz.t[:, 0:w], L.qiT.t[r0:r0 + 64, ih // 2, qt * 128:(qt + 1) * 128], L.kiT2.t[r0:r0 + 64, c_lo:c_lo + w],
                         True, True, [L.qiT.r, L.kiT2.r], [pz.r])
                    ev2(Rh[ih].t[:, 0:w], pz.t[:, 0:w], [pz.r], [Rh[ih].r], relu=True)
                pi = PS[2]
                for ih in range(8):
                    P.mm(pi.t[:, 0:w], dg[b].t[:, ih, :], Rh[ih].t[:, 0:w], ih == 0, ih == 7, [dg[b].r, Rh[ih].r], [pi.r])
                ev2(isc[b].t[:, c_lo:c_lo + w], pi.t[:, 0:w], [pi.r], [isc[b].r])
            P.tt("dve", isc[b].t[:, nk - 128:nk], isc[b].t[:, nk - 128:nk], cmask, ALU.add, [isc[b].r, consts.r], [isc[b].r])
            P.memset("pool", thr[b].t[:], -IDX_LIM, [thr[b].r])
            if qt >= 2:
                P.memset("pool", cnt[b].t[:], 0.0, [cnt[b].r])
                for it in range(IDX_NIT):
                    step = IDX_LIM / (2.0 ** it)
                    P.ts("dve", cand.t[:], thr[b].t[:], step, None, ALU.add, None, [thr[b].r], [cand.r])
                    P.ts("dve", junk.t[:, 0:nk], isc[b].t[:, 0:nk], cand.t[:, 0:1], 0.0, ALU.is_ge, ALU.add,
                         [isc[b].r, cand.r], [junk.r, cnt[b].r], accum_out=cnt[b].t[:, it:it + 1])
                    P.ts("dve", ind.t[:], cnt[b].t[:, it:it + 1], 255.5, step, ALU.is_ge, ALU.mult, [cnt[b].r], [ind.r])
                    P.tt("dve", thr[b].t[:], thr[b].t[:], ind.t[:], ALU.add, [thr[b].r, ind.r], [thr[b].r])
            P.ts("pool", pen[b].t[:, 0:nk], isc[b].t[:, 0:nk], thr[b].t[:, 0:1], -30000.0, ALU.is_lt, ALU.mult,
                 [isc[b].r, thr[b].r], [pen[b].r])

        def att(qi_):
            qt = qtl[qi_]
            b = qi_ % 2
            for hp in range(2):
                touched = set()
                for kb in range(qt + 1):
                    near = kb >= qt - 1
                    for g in range(2):
                        hh0 = hp * 8 + g * 4
                        ps = PS[g]
                        P.mm(ps.t[:, :], L.knT.t[:, kb * 128:(kb + 1) * 128], qnq[b].t[:, hh0 * 128:(hh0 + 4) * 128], True, False,
                             [L.knT.r, qnq[b].r], [ps.r])
                        P.mm(ps.t[:, :], pen[b].t[:, kb * 128:(kb + 1) * 128], irep4, False, not near, [pen[b].r, consts.r], [ps.r])
                        if near:
                            P.mm(ps.t[:, :], ident, biasT.t[:, 0 if kb == qt else 1, hh0 * 128:(hh0 + 4) * 128], False, True,
                                 [consts.r, biasT.r], [ps.r])
                        pt = PTb[ctr["pt"] % 3]
                        ctr["pt"] += 1
                        P.act(pt.t[:], ps.t[:, :], AF.Exp, [ps.r], [pt.r])
                        for i4 in range(4):
                            hd = g * 4 + i4
                            bk = 3 + hd // 3
                            off = (hd % 3) * 129
                            P.mm(PS[bk].t[:, off:off + 129], pt.t[:, i4 * 128:(i4 + 1) * 128], L.vals.t[:, kb, :], bk not in touched, False,
                                 [pt.r, L.vals.r], [PS[bk].r])
                            touched.add(bk)
                for bi in range(3):
                    nh = 3 if bi < 2 else 2
                    P.op("dve", lambda e, bi=bi, nh=nh, hp=hp: e.reciprocal(
                        out=rden.t[:, hp * 8 + bi * 3: hp * 8 + bi * 3 + nh],
                        in_=PS[3 + bi].t[:, 0:nh * 129].rearrange("p (h c) -> p h c", c=129)[:, :, 128]),
                        [PS[3 + bi].r], [rden.r])
                for hd in range(8):
                    bk = 3 + hd // 3
                    off = (hd % 3) * 129
                    gh = hp * 8 + hd
                    if hd % 2 == 0:
                        P.act(olat.t[:, gh * 128:(gh + 1) * 128], PS[bk].t[:, off:off + 128], AF.Identity, [PS[bk].r, rden.r], [olat.r],
                              scale=rden.t[:, gh:gh + 1])
                    else:
                        P.ts("dve", olat.t[:, gh * 128:(gh + 1) * 128], PS[bk].t[:, off:off + 128], rden.t[:, gh:gh + 1], None,
                             ALU.mult, None, [PS[bk].r, rden.r], [olat.r])
            for hd in range(16):
                P.tr(PT[hd // 8].t[:, (hd % 8) * 128:(hd % 8 + 1) * 128], olat.t[:, hd * 128:(hd + 1) * 128], ident,
                     [olat.r, consts.r], [PT[hd // 8].r])
            P.copy("dve", olT.t[:, 0:1024], PT[0].t[:, :], [PT[0].r], [olT.r])
            P.act(olT.t[:, 1024:2048], PT[1].t[:, :], AF.Identity, [PT[1].r], [olT.r])
            for half in range(2):
                bank = PS[2]
                for c4 in range(4):
                    j2 = half * 4 + c4
                    P.mm(bank.t[:, c4 * 128:(c4 + 1) * 128], wuvP.t[:, 2 * j2, :], olT.t[:, (2 * j2) * 128:(2 * j2 + 1) * 128],
                         c4 == 0, False, [wuvP.r, olT.r], [bank.r])
                    P.mm(bank.t[:, c4 * 128:(c4 + 1) * 128], wuvP.t[:, 2 * j2 + 1, :], olT.t[:, (2 * j2 + 1) * 128:(2 * j2 + 2) * 128],
                         False, True, [wuvP.r, olT.r], [bank.r])
                ev2(oTq.t[:, half * 512:(half + 1) * 512], bank.t[:, :], [bank.r], [oTq.r])
            for half in range(2):
                bank = PS[half]
                for c in range(8):
                    P.mm(bank.t[:, :], oTq.t[:, c * 128:(c + 1) * 128], wo.t[:, c, half * 512:(half + 1) * 512], c == 0, c == 7,
                         [oTq.r, wo.r], [bank.r])
                P.tt("dve", xq[b].t[:, half * 512:(half + 1) * 512], bank.t[:, :], xq[b].t[:, half * 512:(half + 1) * 512], ALU.add,
                     [bank.r, xq[b].r], [xq[b].r])
            P.dma("sp", y[qt * 128:(qt + 1) * 128, :], xq[b].t[:], f"st_xq{b}", [xq[b].r], [Ry[qt]])

        idx(0)
        for qi_ in range(len(qtl)):
            if qi_ + 1 < len(qtl):
                idx(qi_ + 1)
            att(qi_)
        P.barrier()
        P.reset(m)

    order = ["P", "A0", "B0", "C0", "F0", "A1", "B1", "F1"]
    upto = order.index(stop) if stop is not None else len(order) - 1
    if n_layers == 1:
        upto = min(upto, order.index("F0"))
    phase_P()
    if dbg:
        dbg_mod = nc.dram_tensor("dbg_mod", [128, 128], F32, kind="ExternalOutput").ap()
        P.dma("sp", dbg_mod[:, 0:96], modfm.t[:], "st_dbg0", [modfm.r], [])
        P.dma("sp", dbg_mod[:, 96:128], gs.t[:], "st_dbg1", [gs.r], [])
    P.barrier()
    P.reset(pers_mark)
    if upto >= 1:
        phase_A0()
    if upto >= 2:
        phase_B0()
    if upto >= 3:
        phase_C(0)
    if upto >= 4:
        phase_F(0)
    if upto >= 5:
        L = alloc_L1()
        phase_A1(L)
    if upto >= 6:
        phase_B1(L)
        P.reset(pers_mark)
    if upto >= 7:
        phase_F(1)
    P.barrier()
    P.emit()
    return nc, P


def make_consts():
    c = np.zeros((128, NCONST), np.float32)
    p = np.arange(128)[:, None]
    q = np.arange(128)[None, :]
    c[:, C_ID:C_ID + 128] = (p == q)
    c[:, C_TRI:C_TRI + 128] = (p < q)
    c[:, C_NEGM:C_NEGM + 128] = np.where(p >= q, -30000.0, 0.0)
    c[:, C_NEGU:C_NEGU + 128] = np.where(p >= q, -1.0, 0.0)
    c[:, C_ONES:C_ONES + 128] = 1.0
    c[:, C_NEGI:C_NEGI + 128] = -1.0 * (p == q)
    c[:, C_CMASK:C_CMASK + 128] = np.where(q > p, -1e30, 0.0)
    for r_ in range(4):
        c[:, C_IREP + r_ * 128:C_IREP + (r_ + 1) * 128] = (p == q)
    return c.astype(ml_dtypes.bfloat16)


def t5_bucket_np(dist):
    n = np.maximum(dist, 0)
    nf = np.maximum(n, 1).astype(np.float32)
    large = 16 + (np.log(nf / 16) / np.float32(math.log(128 / 16)) * 16).astype(np.int32)
    large = np.minimum(large, 31)
    return np.where(n < 16, n, large)


def prep_inputs(inp, cores):
    f = lambda a: np.ascontiguousarray(np.asarray(a, dtype=np.float32))
    x = f(inp["x"])
    c = f(inp["c"])
    ada_w = f(inp["ada_w"])
    ada_b = f(inp["ada_b"])
    ada_bT = np.ascontiguousarray(ada_b.reshape(2, 48, 128).transpose(2, 0, 1).reshape(128, 96))
    nmixT = np.ascontiguousarray(f(inp["norm_mix"]).reshape(2, 8, 128).transpose(2, 0, 1).reshape(128, 16))
    nffnT = np.ascontiguousarray(f(inp["norm_ffn"]).reshape(2, 8, 128).transpose(2, 0, 1).reshape(128, 16))
    cw = f(inp["ffn_conv_w"])
    cb = f(inp["ffn_conv_b"])
    conv = np.concatenate([cw, cb[:, None, :]], axis=1)
    convT = np.ascontiguousarray(conv.reshape(2, 4, NCH, 128).transpose(3, 0, 2, 1).reshape(128, 2 * NCH * 4))
    rb = f(inp["rel_bias"])
    p_ = np.arange(128)[:, None]
    q_ = np.arange(128)[None, :]
    bg = np.zeros((128, 2, 16, 128), np.float32)
    for w_ in range(2):
        bk = t5_bucket_np(w_ * 128 + q_ - p_)
        bg[:, w_, :, :] = rb[bk].transpose(0, 2, 1)
    bc = np.broadcast_to(rb[31][None, None, :, None], (128, 2, 16, 128))
    shared = {
        "ada_w": ada_w, "ada_b": ada_b, "ada_bT": ada_bT, "nmixT": nmixT, "nffnT": nffnT,
        "sb_w_in": f(inp["sb_w_in"])[0], "sb_w_out": f(inp["sb_w_out"])[0], "dsa_w_out": f(inp["dsa_w_out"])[0],
        "ffn_w_up": f(inp["ffn_w_up"]), "ffn_w_down": f(inp["ffn_w_down"]),
        "convT": convT, "consts": make_consts(),
        "dsa_w_in": f(inp["dsa_w_in"])[0], "dsa_w_uv": f(inp["dsa_w_uv"])[0],
        "dsa_gT": np.ascontiguousarray(np.stack([f(inp["dsa_q_norm"])[0], f(inp["dsa_k_norm"])[0]], axis=1)),
        "biasg": np.ascontiguousarray(bg.reshape(128, 4096)), "biasc": np.ascontiguousarray(bc.reshape(128, 4096)),
    }
    maps = []
    for b in cores:
        d = dict(shared)
        d["x"] = x[b]
        d["cT"] = np.ascontiguousarray(c[b].reshape(8, 128).T)
        maps.append(d)
    return maps


_CACHE = {}


def kernel(**inputs):
    if "nc" not in _CACHE:
        _CACHE["nc"] = build()[0]
    nc = _CACHE["nc"]
    maps = prep_inputs(inputs, range(8))
    res = run_bass_kernel_spmd(nc, maps, core_ids=list(range(8)))
    out = np.stack([np.asarray(r["y"]) for r in res.results], axis=0)
    return out.astype(np.float32)
```

```python
import math
from contextlib import ExitStack

import numpy as np
import ml_dtypes

import concourse.bass as bass
import concourse.mybir as mybir
from concourse.bass_utils import run_bass_kernel_spmd

F32 = mybir.dt.float32
BF16 = mybir.dt.bfloat16
ALU = mybir.AluOpType
AF = mybir.ActivationFunctionType

S = 4096
D = 1024
NT = S // 128
FFN = 2816
NCH = 2 * FFN // 128
EPS = 1e-6
SB_BASE = 16512
SB_TOP = 229344

C_ID, C_TRI, C_NEGM, C_NEGU, C_ONES, C_NEGI, C_CMASK, C_IREP = 0, 128, 256, 384, 512, 640, 768, 896
NCONST = 896 + 512
IDX_LIM = 128.0
IDX_NIT = 24


class Region:
    __slots__ = ("w", "r")

    def __init__(self):
        self.w = None
        self.r = {}


class Prog:
    ENG = ("pe", "act", "dve", "pool", "sp")

    def __init__(self, nc):
        self.nc = nc
        self.streams = {e: [] for e in self.ENG}
        self.count = {}
        self.waited = {e: {} for e in self.ENG}
        self.cur = SB_BASE
        self.nalloc = 0
        self.nwaits = 0

    def alloc(self, shape, dtype, name="t"):
        nbytes = int(np.prod(shape[1:])) * (4 if dtype == F32 else 2)
        nbytes = (nbytes + 63) // 64 * 64
        off = self.cur
        assert off + nbytes <= SB_TOP, f"SBUF overflow allocating {name} {shape}"
        self.cur += nbytes
        self.nalloc += 1
        return self.nc.alloc_sbuf_tensor_at(f"{name}_{self.nalloc}", list(shape), dtype, offset=off)

    def mark(self):
        return self.cur

    def reset(self, m):
        self.cur = m

    def _need(self, eng, waits, tok):
        if tok is None:
            return
        k, v = tok
        if self.waited[eng].get(k, 0) >= v:
            return
        if waits.get(k, 0) < v:
            waits[k] = v

    def _deps(self, eng, reads, writes):
        waits = {}
        for r in reads:
            if r.w is not None:
                if not (eng == "pe" and r.w[0] == "pe"):
                    self._need(eng, waits, r.w)
        for w in writes:
            if w.w is not None and w.w[0] != eng:
                self._need(eng, waits, w.w)
            for k, v in w.r.items():
                if k != eng:
                    self._need(eng, waits, (k, v))
        for k, v in waits.items():
            self.waited[eng][k] = v
        return list(waits.items())

    def _commit(self, tok, reads, writes):
        k, v = tok
        for r in reads:
            if r.r.get(k, 0) < v:
                r.r[k] = v
        for w in writes:
            w.w = tok
            w.r = {}

    def op(self, eng, fn, reads=(), writes=()):
        waits = self._deps(eng, reads, writes)
        self.count[eng] = self.count.get(eng, 0) + 1
        tok = (eng, self.count[eng])
        self.streams[eng].append((waits, fn, eng, 1))
        self.nwaits += len(waits)
        self._commit(tok, reads, writes)

    def dma(self, q, out, in_, sem, reads=(), writes=()):
        waits = self._deps(q, reads, writes)
        self.count[sem] = self.count.get(sem, 0) + 16
        tok = (sem, self.count[sem])
        self.streams[q].append((waits, lambda e: e.dma_start(out=out, in_=in_), sem, 16))
        self.nwaits += len(waits)
        self._commit(tok, reads, writes)

    def barrier(self):
        snap = dict(self.count)
        for e in self.ENG:
            waits = []
            for k, v in snap.items():
                if v > 0 and self.waited[e].get(k, 0) < v:
                    waits.append((k, v))
                    self.waited[e][k] = v
            if waits:
                self.streams[e].append((waits, None, None, 0))

    def emit(self):
        nc = self.nc
        with ExitStack() as es:
            sems = {}
            for k in self.count:
                sems[k] = es.enter_context(nc.semaphore("s_" + k))
            block = es.enter_context(nc.Block())

            def run(stream):
                def f(e):
                    for waits, fn, semk, inc in stream:
                        for k, v in waits:
                            e.wait_ge(sems[k], v)
                        if fn is not None:
                            fn(e).then_inc(sems[semk], inc)
                return f

            block.tensor(run(self.streams["pe"]))
            block.scalar(run(self.streams["act"]))
            block.vector(run(self.streams["dve"]))
            block.gpsimd(run(self.streams["pool"]))
            block.sync(run(self.streams["sp"]))

    def mm(self, out, lhsT, rhs, start, stop, reads, writes):
        self.op("pe", lambda e: e.matmul(out, lhsT, rhs, start=start, stop=stop), reads, writes)

    def tr(self, out, in_, ident, reads, writes):
        self.op("pe", lambda e: e.transpose(out, in_, ident), reads, writes)

    def act(self, out, in_, func, reads, writes, bias=None, scale=None, accum_out=None):
        kw = {}
        if bias is not None:
            kw["bias"] = bias
        if scale is not None:
            kw["scale"] = scale
        if accum_out is not None:
            kw["accum_out"] = accum_out
        self.op("act", lambda e: e.activation(out=out, in_=in_, func=func, **kw), reads, writes)

    def ts(self, eng, out, in0, s1, s2, op0, op1, reads, writes, accum_out=None):
        kw = {}
        if accum_out is not None:
            kw["accum_out"] = accum_out
        if op1 is None:
            self.op(eng, lambda e: e.tensor_scalar(out=out, in0=in0, scalar1=s1, scalar2=None, op0=op0, **kw), reads, writes)
        else:
            self.op(eng, lambda e: e.tensor_scalar(out=out, in0=in0, scalar1=s1, scalar2=s2, op0=op0, op1=op1, **kw), reads, writes)

    def tt(self, eng, out, in0, in1, op, reads, writes):
        self.op(eng, lambda e: e.tensor_tensor(out=out, in0=in0, in1=in1, op=op), reads, writes)

    def stt(self, eng, out, in0, scalar, in1, op0, op1, reads, writes):
        self.op(eng, lambda e: e.scalar_tensor_tensor(out=out, in0=in0, scalar=scalar, in1=in1, op0=op0, op1=op1), reads, writes)

    def copy(self, eng, out, in_, reads, writes):
        self.op(eng, lambda e: e.tensor_copy(out=out, in_=in_), reads, writes)

    def memset(self, eng, ap, val, writes):
        self.op(eng, lambda e: e.memset(ap, val), (), writes)


class T:
    def __init__(self, t, n=1):
        self.t = t
        self.R = [Region() for _ in range(n)]
        self.r = self.R[0]


def build(n_layers=2, dbg=False, stop=None):
    nc = bass.Bass("TRN2", target_bir_lowering=False)
    P = Prog(nc)

    def din(name, shape, dt=F32):
        return nc.dram_tensor(name, list(shape), dt, kind="ExternalInput").ap()

    def dscr(name, shape, dt=BF16):
        kind = "ExternalOutput" if dbg else "Internal"
        return nc.dram_tensor(name, list(shape), dt, kind=kind).ap()

    x_in = din("x", [S, D])
    cT_in = din("cT", [128, 8])
    ada_w = din("ada_w", [2, D, 6 * D])
    ada_b = din("ada_b", [2, 6 * D])
    ada_bT = din("ada_bT", [128, 96])
    nmixT = din("nmixT", [128, 16])
    nffnT = din("nffnT", [128, 16])
    sb_w_in = din("sb_w_in", [D, 3 * D])
    sb_w_out = din("sb_w_out", [D, D])
    dsa_w_out_in = din("dsa_w_out", [D, D])
    ffn_w_up = din("ffn_w_up", [2, D, 2 * FFN])
    ffn_w_down = din("ffn_w_down", [2, FFN, D])
    convT = din("convT", [128, 2 * NCH * 4])
    consts_in = din("consts", [128, NCONST], BF16)
    dsa_w_in = din("dsa_w_in", [D, 2760])
    dsa_gT = din("dsa_gT", [128, 2])
    dsa_w_uv = din("dsa_w_uv", [16, 128, 64])
    biasg_in = din("biasg", [128, 4096])
    biasc_in = din("biasc", [128, 4096])
    osel_in = din("osel", [128, 4096], BF16)
    sm_in = din("smc", [128, 96], BF16)
    nsel_in = din("nsel", [32, 4096], BF16)
    y = nc.dram_tensor("y", [S, D], F32, kind="ExternalOutput").ap()

    qkT_s = dscr("qkT_s", [16, 128, S])
    v_s = dscr("v_s", [S, D])
    oT_s = dscr("oT_s", [D, S])
    qn_s = dscr("qn_s", [NT, 128, 2048])
    wo_s = dscr("wo_s", [2, 128, 8, D])
    wd_s = dscr("wd_s", [2, 128, 22, D])

    Ry = [Region() for _ in range(NT)]
    Rqk = [[Region() for _ in range(8)] for _ in range(16)]
    Rv = [Region() for _ in range(NT)]
    Ro = [[Region() for _ in range(8)] for _ in range(16)]
    Rqn = [Region() for _ in range(NT)]
    Rwo = [Region(), Region()]
    Rwd = [Region(), Region()]

    PS = [T(nc.alloc_psum_tensor(f"ps{i}", [128, 512], F32)) for i in range(6)]
    PT = [T(nc.alloc_psum_tensor(f"pt{i}", [128, 1024], BF16)) for i in range(2)]

    consts = T(P.alloc([128, NCONST], BF16, "consts"))
    cst = consts.t
    ident = cst[:, C_ID:C_ID + 128]
    tri = cst[:, C_TRI:C_TRI + 128]
    negm = cst[:, C_NEGM:C_NEGM + 128]
    negU = cst[:, C_NEGU:C_NEGU + 128]
    ones = cst[:, C_ONES:C_ONES + 128]
    negI = cst[:, C_NEGI:C_NEGI + 128]
    cmask = cst[:, C_CMASK:C_CMASK + 128]
    irep4 = cst[:, C_IREP:C_IREP + 512]
    modfm = T(P.alloc([128, 96], F32, "modfm"))
    gs = T(P.alloc([128, 32], F32, "gs"))
    convw = T(P.alloc([128, 2 * NCH * 4], F32, "convw"))
    P.dma("sp", consts.t[:], consts_in[:, :], "ld_c0", (), [consts.r])
    P.dma("sp", convw.t[:], convT[:, :], "ld_c1", (), [convw.r])
    pers_mark = P.mark()

    def sh_ap(layer, which, c):
        col = layer * 48 + (0 if which == 0 else 24) + c
        return modfm.t[:, col:col + 1]

    def gs_ap(layer, which, c):
        col = layer * 16 + which * 8 + c
        return gs.t[:, col:col + 1]

    def phase_P():
        cT = T(P.alloc([128, 8], F32, "cT"))
        cond = T(P.alloc([128, 8], F32, "cond"))
        condb = T(P.alloc([128, 8], BF16, "condb"))
        crep = T(P.alloc([128, 8, 128], BF16, "crep"))
        abT = T(P.alloc([128, 96], F32, "abT"))
        nmx = T(P.alloc([128, 16], F32, "nmx"))
        nff = T(P.alloc([128, 16], F32, "nff"))
        P.dma("sp", cT.t[:], cT_in[:, :], "ld_p0", (), [cT.r])
        P.dma("sp", abT.t[:], ada_bT[:, :], "ld_p1", (), [abT.r])
        P.dma("sp", nmx.t[:], nmixT[:, :], "ld_p2", (), [nmx.r])
        P.dma("sp", nff.t[:], nffnT[:, :], "ld_p3", (), [nff.r])
        P.act(cond.t[:], cT.t[:], AF.Silu, [cT.r], [cond.r])
        P.copy("dve", condb.t[:], cond.t[:], [cond.r], [condb.r])
        for kc in range(8):
            P.ts("dve", crep.t[:, kc, :], ones, cond.t[:, kc:kc + 1], None, ALU.mult, None,
                 [consts.r, cond.r], [crep.r])
        wp = [T(P.alloc([128, 8, 1024], BF16, "adawp")) for _ in range(2)]
        abb = [T(P.alloc([128, 1024], F32, "abb")) for _ in range(2)]
        gbc = [T(P.alloc([128, 1024], F32, "gbc")) for _ in range(2)]
        wst = [T(P.alloc([128, 1024], F32, "wst")) for _ in range(3)]
        wob = [T(P.alloc([128, 1024], BF16, "wob")) for _ in range(3)]
        PM = PS[0]
        npiece = 0
        nst = 0
        for layer in range(n_layers):
            for g in range(6):
                w = wp[npiece % 2]
                for kc in range(8):
                    P.dma("pool", w.t[:, kc, :], ada_w[layer, kc * 128:(kc + 1) * 128, g * 1024:(g + 1) * 1024],
                          f"ld_aw{npiece % 2}", (), [w.r])
                npiece += 1
                if g in (0, 1, 3, 4):
                    for oc in range(8):
                        col = layer * 48 + g * 8 + oc
                        for kc in range(8):
                            P.mm(PM.t[:, col:col + 1], w.t[:, kc, oc * 128:(oc + 1) * 128], condb.t[:, kc:kc + 1],
                                 kc == 0, kc == 7, [w.r, condb.r], [PM.r])
                else:
                    which = 0 if g == 2 else 1
                    ab = abb[which]
                    P.dma("sp", ab.t[:], ada_b[layer:layer + 1, g * 1024:(g + 1) * 1024].partition_broadcast(128),
                          f"ld_abb{which}", (), [ab.r])
                    gb = gbc[which]
                    for half in range(2):
                        PG = PS[1 + half]
                        for kc in range(8):
                            P.mm(PG.t[:, :], crep.t[:, kc, :], w.t[:, kc, half * 512:(half + 1) * 512],
                                 kc == 0, kc == 7, [w.r, crep.r], [PG.r])
                        P.tt("dve", gb.t[:, half * 512:(half + 1) * 512], PG.t[:, :], ab.t[:, half * 512:(half + 1) * 512],
                             ALU.add, [PG.r, ab.r], [gb.r])
                    if which == 0:
                        srcs = [(sb_w_out if layer == 0 else dsa_w_out_in)[c * 128:(c + 1) * 128, :] for c in range(8)]
                        dsts = [wo_s[layer, :, c, :] for c in range(8)]
                        Rdst = Rwo[layer]
                    else:
                        srcs = [ffn_w_down[layer, c * 128:(c + 1) * 128, :] for c in range(22)]
                        dsts = [wd_s[layer, :, c, :] for c in range(22)]
                        Rdst = Rwd[layer]
                    for sa, da in zip(srcs, dsts):
                        ws = wst[nst % 3]
                        wb = wob[nst % 3]
                        P.dma("sp", ws.t[:], sa, f"ld_wst{nst % 3}", (), [ws.r])
                        P.tt("dve" if nst % 2 == 0 else "pool", wb.t[:], ws.t[:], gb.t[:], ALU.mult, [ws.r, gb.r], [wb.r])
                        P.dma("sp", da, wb.t[:], f"st_wob{nst % 3}", [wb.r], [Rdst])
                        nst += 1
        ncol = 48 * n_layers
        P.tt("dve", modfm.t[:, 0:ncol], PM.t[:, 0:ncol], abT.t[:, 0:ncol], ALU.add, [PM.r, abT.r], [modfm.r])
        for layer in range(n_layers):
            for which in range(2):
                sc0 = layer * 48 + (8 if which == 0 else 32)
                nrm = (nmx if which == 0 else nff).t[:, layer * 8:(layer + 1) * 8]
                P.stt("dve", gs.t[:, layer * 16 + which * 8: layer * 16 + which * 8 + 8],
                      modfm.t[:, sc0:sc0 + 8], 1.0, nrm, ALU.add, ALU.mult, [modfm.r, nmx.r, nff.r], [gs.r])


    class NormBufs:
        def __init__(self, ntt):
            self.ntt = ntt
            self.junk = T(P.alloc([128, 1024], BF16, "junk"))
            self.ss = T(P.alloc([128, 4], F32, "ss"))
            self.rs = T(P.alloc([128, 4], F32, "rs"))
            self.rstd = T(P.alloc([128, 4], F32, "rstd"))
            self.xn = T(P.alloc([128, ntt, 1024], BF16, "xn"))
            self.n = 0

    def emit_norm_T(nb, xs, hT, hoff, layer, which):
        ntt = nb.ntt
        P.memset("pool", nb.ss.t[:], 0.0, [nb.ss.r])
        for j in range(ntt):
            P.act(nb.junk.t[:], xs.t[:, j, :], AF.Square, [xs.r], [nb.junk.r, nb.ss.r], accum_out=nb.ss.t[:, j:j + 1])
        P.act(nb.rs.t[:, 0:ntt], nb.ss.t[:, 0:ntt], AF.Sqrt, [nb.ss.r], [nb.rs.r], bias=EPS, scale=1.0 / D)
        P.op("dve", lambda e: e.reciprocal(out=nb.rstd.t[:, 0:ntt], in_=nb.rs.t[:, 0:ntt]), [nb.rs.r], [nb.rstd.r])
        for j in range(ntt):
            P.ts("dve" if j % 2 == 0 else "pool", nb.xn.t[:, j, :], xs.t[:, j, :], nb.rstd.t[:, j:j + 1], None,
                 ALU.mult, None, [xs.r, nb.rstd.r], [nb.xn.r])
        W = ntt * 128
        per = 1024 // W
        c = 0
        while c < 8:
            pt = PT[nb.n % 2]
            nb.n += 1
            for cc in range(per):
                for j in range(ntt):
                    P.tr(pt.t[:, cc * W + j * 128: cc * W + (j + 1) * 128], nb.xn.t[:, j, (c + cc) * 128:(c + cc + 1) * 128],
                         ident, [nb.xn.r, consts.r], [pt.r])
            for cc in range(per):
                ch = c + cc
                if ch % 2 == 0:
                    P.act(hT.t[:, ch, hoff:hoff + W], pt.t[:, cc * W:(cc + 1) * W], AF.Identity, [pt.r, gs.r, modfm.r], [hT.r],
                          bias=sh_ap(layer, which, ch), scale=gs_ap(layer, which, ch))
                else:
                    P.ts("dve", hT.t[:, ch, hoff:hoff + W], pt.t[:, cc * W:(cc + 1) * W], gs_ap(layer, which, ch),
                         sh_ap(layer, which, ch), ALU.mult, ALU.add, [pt.r, gs.r, modfm.r], [hT.r])
            c += per

    def x_src(layer):
        return x_in if layer == 0 else y

    def phase_A0():
        m = P.mark()
        win = T(P.alloc([128, 8, 3 * D], BF16, "win"))
        for kc in range(8):
            P.dma("pool", win.t[:, kc, :], sb_w_in[kc * 128:(kc + 1) * 128, :], "ld_win", (), [win.r])
        xs = [T(P.alloc([128, 4, 1024], F32, "xs")) for _ in range(2)]
        nb = NormBufs(4)
        hT = [T(P.alloc([128, 8, 512], BF16, "hT")) for _ in range(2)]
        qk = [T(P.alloc([128, 512], BF16, "qk")) for _ in range(4)]
        vs = [T(P.alloc([128, 1024], BF16, "vs")) for _ in range(2)]

        def load(Tq):
            xb = xs[Tq % 2]
            P.dma("sp", xb.t[:], x_in[Tq * 512:(Tq + 1) * 512, :].rearrange("(j p) d -> p j d", p=128),
                  f"ld_xs{Tq % 2}", [], [xb.r])

        load(0)
        nev = 0
        for Tq in range(8):
            if Tq + 1 < 8:
                load(Tq + 1)
            xb = xs[Tq % 2]
            h = hT[Tq % 2]
            emit_norm_T(nb, xb, h, 0, 0, 0)
            for oc in range(16):
                ps = PS[oc % 4]
                for kc in range(8):
                    P.mm(ps.t[:, :], win.t[:, kc, oc * 128:(oc + 1) * 128], h.t[:, kc, :], kc == 0, kc == 7,
                         [win.r, h.r], [ps.r])
                qb = qk[oc % 4]
                sc = 0.125 if oc < 8 else 1.0
                if nev % 2 == 0:
                    P.act(qb.t[:], ps.t[:, :], AF.Identity, [ps.r], [qb.r], scale=sc)
                else:
                    P.ts("dve", qb.t[:], ps.t[:, :], sc, None, ALU.mult, None, [ps.r], [qb.r])
                nev += 1
                P.dma("sp", qkT_s[oc, :, Tq * 512:(Tq + 1) * 512], qb.t[:], f"st_qk{oc % 4}", [qb.r],
                      [Rqk[oc][Tq]])
            for j in range(4):
                vb = vs[j % 2]
                for half in range(2):
                    ps = PS[4 + half]
                    for kc in range(8):
                        P.mm(ps.t[:, :], h.t[:, kc, j * 128:(j + 1) * 128], win.t[:, kc, 2048 + half * 512: 2048 + (half + 1) * 512],
                             kc == 0, kc == 7, [win.r, h.r], [ps.r])
                    if nev % 2 == 0:
                        P.act(vb.t[:, half * 512:(half + 1) * 512], ps.t[:, :], AF.Identity, [ps.r], [vb.r])
                    else:
                        P.copy("dve", vb.t[:, half * 512:(half + 1) * 512], ps.t[:, :], [ps.r], [vb.r])
                    nev += 1
                tile = Tq * 4 + j
                P.dma("sp", v_s[tile * 128:(tile + 1) * 128, :], vb.t[:], f"st_v{j % 2}", [vb.r], [Rv[tile]])
        P.barrier()
        P.reset(m)

    def phase_B0(heads=range(16), Qs=range(8)):
        m = P.mark()
        kT = [T(P.alloc([96, S], BF16, "kT")) for _ in range(2)]
        vh = [T(P.alloc([128, NT, 64], BF16, "vh")) for _ in range(2)]
        qT = [T(P.alloc([96, 512], BF16, "qT"), n=2) for _ in range(3)]
        u = [T(P.alloc([128, 512], F32, "u")) for _ in range(2)]
        spall = [T(P.alloc([128, NT, 512], BF16, "spall"), n=NT) for _ in range(2)]
        A = [T(P.alloc([128, 512], BF16, "A")) for _ in range(3)]
        osb = [T(P.alloc([64, 512], BF16, "osb")) for _ in range(2)]
        csb = [T(P.alloc([128, 512], BF16, "csb")) for _ in range(2)]
        osel = T(P.alloc([128, 32 * 128], BF16, "osel"))
        ltm = T(P.alloc([128, 96], BF16, "ltm"))
        P.dma("sp", osel.t[:], osel_in[:, :], "ld_osel", (), [osel.r])
        P.dma("sp", ltm.t[:], sm_in[:, :], "ld_sm", (), [ltm.r])
        for i_ in range(2):
            P.dma("sp", kT[i_].t[64:96, :], nsel_in[:, :], f"ld_ns{i_}", (), [kT[i_].r])
        PZ = [PS[0], PS[1]]
        PC = [PS[2], PS[3]]
        PL = PS[4]
        PO = PS[5]
        work = [(h, Q) for h in heads for Q in Qs]
        hlist = list(heads)

        def load_head(hi, h):
            kb_ = kT[hi % 2]
            vb_ = vh[hi % 2]
            oc = 8 + h // 2
            r0 = (h % 2) * 64
            P.dma("sp", kb_.t[0:64, :], qkT_s[oc, r0:r0 + 64, :], f"ld_kT{hi % 2}", [Rqk[oc][t_] for t_ in range(8)], [kb_.r])
            for q4 in range(4):
                P.dma("sp", vb_.t[:, q4 * 8:(q4 + 1) * 8, :],
                      v_s[q4 * 1024:(q4 + 1) * 1024, h * 64:(h + 1) * 64].rearrange("(k p) d -> p k d", p=128),
                      f"ld_vh{hi % 2}", Rv, [vb_.r])

        def load_q(wi):
            h, Q = work[wi]
            qb = qT[wi % 3]
            oc = h // 2
            r0 = (h % 2) * 64
            P.dma("sp", qb.t[0:64, :], qkT_s[oc, r0:r0 + 64, Q * 512:(Q + 1) * 512], f"ld_qT{wi % 3}", [Rqk[oc][Q]],
                  [qb.R[0], qb.R[1]])

        cz = [0]
        ca = [0]

        def geom(Q, i):
            kb = 4 * Q + 3 - i
            j = kb - 4 * Q
            c0 = max(j, 0) * 128
            return kb, j, c0

        def pass1(wi):
            h, Q = work[wi]
            hi = hlist.index(h)
            kb_ = kT[hi % 2]
            qb = qT[wi % 3]
            spb = spall[wi % 2]
            n = 4 * Q + 4

            def zmm(i):
                kb, j, c0 = geom(Q, i)
                pz = PZ[(cz[0] + i) % 2]
                P.mm(pz.t[:, c0:512], kb_.t[0:64, kb * 128:(kb + 1) * 128], qb.t[0:64, c0:512], True, True, [kb_.r, qb.R[0]], [pz.r])

            def rest(i):
                kb, j, c0 = geom(Q, i)
                bz = (cz[0] + i) % 2
                pz = PZ[bz]
                P.act(u[bz].t[:, c0:512], pz.t[:, c0:512], AF.Exp, [pz.r], [u[bz].r])
                P.act(spb.t[:, i, c0:512], u[bz].t[:, c0:512], AF.Ln, [u[bz].r], [spb.R[i]], bias=1.0)
                if j >= 0:
                    P.tt("dve", spb.t[:, i, c0:c0 + 128], spb.t[:, i, c0:c0 + 128], tri, ALU.mult, [spb.R[i], consts.r], [spb.R[i]])
                P.mm(PL.t[:, c0:512], osel.t[:, kb * 128:(kb + 1) * 128], spb.t[:, i, c0:512], i == 0, i == n - 1,
                     [osel.r, spb.R[i]], [PL.r])

            zmm(0)
            for i in range(n):
                if i + 1 < n:
                    zmm(i + 1)
                rest(i)
            cz[0] += n
            cb = csb[wi % 2]
            P.copy("dve", cb.t[:], PL.t[:, :], [PL.r], [cb.r])
            P.mm(PL.t[0:96, :], ltm.t[:, :], cb.t[:], True, True, [ltm.r, cb.r], [PL.r])
            P.copy("dve", qb.t[64:96, :], PL.t[64:96, :], [PL.r], [qb.R[1]])

        def pass2(wi):
            h, Q = work[wi]
            hi = hlist.index(h)
            kb_ = kT[hi % 2]
            vb_ = vh[hi % 2]
            qb = qT[wi % 3]
            spb = spall[wi % 2]
            n = 4 * Q + 4

            def cmm(i):
                kb, j, c0 = geom(Q, i)
                pc = PC[(ca[0] + i) % 2]
                P.mm(pc.t[:, c0:512], kb_.t[:, kb * 128:(kb + 1) * 128], qb.t[:, c0:512], True, False, [kb_.r, qb.R[0], qb.R[1]], [pc.r])
                P.mm(pc.t[:, c0:512], negU, spb.t[:, i, c0:512], False, not (j >= 0), [consts.r, spb.R[i]], [pc.r])
                if j >= 0:
                    P.mm(pc.t[:, c0:c0 + 128], ident, negm, False, True, [consts.r], [pc.r])

            def rest(i):
                kb, j, c0 = geom(Q, i)
                pc = PC[(ca[0] + i) % 2]
                ab = A[(ca[0] + i) % 3]
                P.act(ab.t[:, c0:512], pc.t[:, c0:512], AF.Exp, [pc.r], [ab.r])
                P.mm(PO.t[0:64, c0:512], vb_.t[:, kb, :], ab.t[:, c0:512], i == 0, i == n - 1, [vb_.r, ab.r], [PO.r])

            cmm(0)
            for i in range(n):
                if i + 1 < n:
                    cmm(i + 1)
                rest(i)
            ca[0] += n
            ob = osb[wi % 2]
            P.copy("dve", ob.t[:], PO.t[0:64, :], [PO.r], [ob.r])
            P.dma("sp", oT_s[h * 64:(h + 1) * 64, Q * 512:(Q + 1) * 512], ob.t[:], f"st_o{wi % 2}", [ob.r], [Ro[h][Q]])

        load_head(0, hlist[0])
        load_q(0)
        nW = len(work)
        for wi in range(nW + 1):
            if wi < nW:
                if wi + 1 < nW:
                    load_q(wi + 1)
                pass1(wi)
            if wi >= 1:
                pass2(wi - 1)
            if wi < nW:
                h, Q = work[wi]
                hi = hlist.index(h)
                if Q == list(Qs)[0] and hi + 1 < len(hlist):
                    load_head(hi + 1, hlist[hi + 1])
        P.barrier()
        P.reset(m)

    def phase_C(layer):
        m = P.mark()
        wo = T(P.alloc([128, 8, D], BF16, "wo"))
        P.dma("sp", wo.t[:], wo_s[layer], "ld_wo", [Rwo[layer]], [wo.r])
        ob = [T(P.alloc([128, 8, 512], BF16, "ob")) for _ in range(2)]
        xs = [T(P.alloc([128, 4, 1024], F32, "xs")) for _ in range(2)]
        src = x_src(layer)

        def load(Tq):
            o_ = ob[Tq % 2]
            x_ = xs[Tq % 2]
            P.dma("sp", o_.t[:], oT_s[:, Tq * 512:(Tq + 1) * 512].rearrange("(c p) t -> p c t", p=128), f"ld_ob{Tq % 2}",
                  [Ro[h][Tq] for h in range(16)], [o_.r])
            P.dma("sp", x_.t[:], src[Tq * 512:(Tq + 1) * 512, :].rearrange("(j p) d -> p j d", p=128), f"ld_xs{Tq % 2}",
                  [Ry[Tq * 4 + j] for j in range(4)] if layer > 0 else [], [x_.r])

        load(0)
        nps = 0
        for Tq in range(8):
            if Tq + 1 < 8:
                load(Tq + 1)
            o_ = ob[Tq % 2]
            x_ = xs[Tq % 2]
            for j in range(4):
                for half in range(2):
                    ps = PS[nps % 4]
                    nps += 1
                    for kc in range(8):
                        P.mm(ps.t[:, :], o_.t[:, kc, j * 128:(j + 1) * 128], wo.t[:, kc, half * 512:(half + 1) * 512],
                             kc == 0, kc == 7, [o_.r, wo.r], [ps.r])
                    P.tt("dve", x_.t[:, j, half * 512:(half + 1) * 512], ps.t[:, :], x_.t[:, j, half * 512:(half + 1) * 512],
                         ALU.add, [ps.r, x_.r], [x_.r])
            P.dma("sp", y[Tq * 512:(Tq + 1) * 512, :].rearrange("(j p) d -> p j d", p=128), x_.t[:], f"st_xs{Tq % 2}",
                  [x_.r], [Ry[Tq * 4 + j] for j in range(4)])
        P.barrier()
        P.reset(m)

    def phase_F(layer):
        m = P.mark()
        TT = 2
        W = TT * 128
        NS = S // W
        wup = T(P.alloc([128, 8, 2 * FFN], BF16, "wup"))
        for kc in range(8):
            P.dma("pool", wup.t[:, kc, :], ffn_w_up[layer, kc * 128:(kc + 1) * 128, :], "ld_wup", (), [wup.r])
        wd = T(P.alloc([128, 22, D], BF16, "wd"))
        P.dma("sp", wd.t[:], wd_s[layer], "ld_wd", [Rwd[layer]], [wd.r])
        xs = [T(P.alloc([128, TT, 1024], F32, "xs")) for _ in range(2)]
        nb = NormBufs(TT)
        hT = [T(P.alloc([128, 8, W + 2], BF16, "hT")) for _ in range(2)]
        gv = T(P.alloc([128, 22, W], BF16, "gv"))
        ya = [T(P.alloc([128, W], F32, "ya")) for _ in range(4)]
        yb = [T(P.alloc([128, W], F32, "yb")) for _ in range(4)]
        sg = [T(P.alloc([128, W], F32, "sg")) for _ in range(2)]

        def cw(ch, k):
            col = (layer * NCH + ch) * 4 + k
            return convw.t[:, col:col + 1]

        def load(Ti):
            x_ = xs[Ti % 2]
            P.dma("sp", x_.t[:], y[Ti * W:(Ti + 1) * W, :].rearrange("(j p) d -> p j d", p=128), f"ld_xs{Ti % 2}",
                  [Ry[Ti * TT + j] for j in range(TT)], [x_.r])

        load(0)
        P.memset("pool", hT[0].t[:, :, 0:2], 0.0, [hT[0].r])
        nu = 0
        nd = 0
        for Ti in range(NS):
            if Ti + 1 < NS:
                load(Ti + 1)
            x_ = xs[Ti % 2]
            h = hT[Ti % 2]
            if Ti > 0:
                hp = hT[(Ti - 1) % 2]
                P.copy("pool", h.t[:, :, 0:2], hp.t[:, :, W:W + 2], [hp.r], [h.r])
            emit_norm_T(nb, x_, h, 2, layer, 1)
            for mch in range(22):
                outs = []
                for side in range(2):
                    ch = mch + 22 * side
                    ps = PS[nu % 4]
                    a = ya[nu % 4]
                    b = yb[nu % 4]
                    nu += 1
                    for kc in range(8):
                        P.mm(ps.t[:, 0:W + 2], wup.t[:, kc, ch * 128:(ch + 1) * 128], h.t[:, kc, :], kc == 0, kc == 7,
                             [wup.r, h.r], [ps.r])
                    P.act(a.t[:], ps.t[:, 2:W + 2], AF.Identity, [ps.r, convw.r], [a.r], bias=cw(ch, 3), scale=cw(ch, 2))
                    P.stt("dve", b.t[:], ps.t[:, 1:W + 1], cw(ch, 1), a.t[:], ALU.mult, ALU.add, [ps.r, a.r, convw.r], [b.r])
                    P.stt("dve", a.t[:], ps.t[:, 0:W], cw(ch, 0), b.t[:], ALU.mult, ALU.add, [ps.r, b.r, convw.r], [a.r])
                    outs.append(a)
                s_ = sg[mch % 2]
                P.act(s_.t[:], outs[0].t[:], AF.Silu, [outs[0].r], [s_.r])
                P.tt("pool", gv.t[:, mch, :], s_.t[:], outs[1].t[:], ALU.mult, [s_.r, outs[1].r], [gv.r])
            for j in range(TT):
                for half in range(2):
                    ps = PS[4 + nd % 2]
                    nd += 1
                    for mch in range(22):
                        P.mm(ps.t[:, :], gv.t[:, mch, j * 128:(j + 1) * 128], wd.t[:, mch, half * 512:(half + 1) * 512],
                             mch == 0, mch == 21, [gv.r, wd.r], [ps.r])
                    P.tt("dve", x_.t[:, j, half * 512:(half + 1) * 512], ps.t[:, :], x_.t[:, j, half * 512:(half + 1) * 512],
                         ALU.add, [ps.r, x_.r], [x_.r])
            P.dma("sp", y[Ti * W:(Ti + 1) * W, :].rearrange("(j p) d -> p j d", p=128), x_.t[:], f"st_xs{Ti % 2}",
                  [x_.r], [Ry[Ti * TT + j] for j in range(TT)])
        P.barrier()
        P.reset(m)


    class L1:
        pass

    def alloc_L1():
        L = L1()
        L.knT = T(P.alloc([128, S], BF16, "knT"))
        L.vals = T(P.alloc([128, NT, 129], BF16, "vals"))
        L.kiT2 = T(P.alloc([128, S], BF16, "kiT2"))
        L.qiT = T(P.alloc([128, 4, S], BF16, "qiT"))
        L.wi = T(P.alloc([128, NT, 8], F32, "wi"))
        L.gq = T(P.alloc([128, 2], F32, "gq"))
        return L

    def phase_A1(L):
        m = P.mark()
        win = T(P.alloc([128, 8, 2760], BF16, "win1"))
        for kc in range(8):
            P.dma("pool", win.t[:, kc, :], dsa_w_in[kc * 128:(kc + 1) * 128, :], "ld_win", (), [win.r])
        wki2 = T(P.alloc([128, 8, 128], BF16, "wki2"))
        P.copy("dve", wki2.t[:, :, 0:64], win.t[:, :, 2688:2752], [win.r], [wki2.r])
        P.copy("dve", wki2.t[:, :, 64:128], win.t[:, :, 2688:2752], [win.r], [wki2.r])
        gT = T(P.alloc([128, 2], F32, "gT"))
        P.dma("sp", gT.t[:], dsa_gT[:, :], "ld_gT", (), [gT.r])
        P.ts("dve", L.gq.t[:, 0:1], gT.t[:, 0:1], 128.0 ** -0.5, None, ALU.mult, None, [gT.r], [L.gq.r])
        P.copy("dve", L.gq.t[:, 1:2], gT.t[:, 1:2], [gT.r], [L.gq.r])
        P.memset("pool", L.vals.t[:], 1.0, [L.vals.r])
        xs = [T(P.alloc([128, 4, 1024], F32, "xs")) for _ in range(2)]
        nb = NormBufs(4)
        hT = [T(P.alloc([128, 8, 512], BF16, "hT")) for _ in range(2)]
        qnb = T(P.alloc([128, 4, 16, 128], BF16, "qnb"))
        SQ = [T(P.alloc([128, 512], BF16, "sq")) for _ in range(2)]
        RS = [T(P.alloc([128, 512], F32, "rs")) for _ in range(2)]

        def load(Tq):
            xb = xs[Tq % 2]
            P.dma("sp", xb.t[:], y[Tq * 512:(Tq + 1) * 512, :].rearrange("(j p) d -> p j d", p=128),
                  f"ld_xs{Tq % 2}", [Ry[Tq * 4 + j] for j in range(4)], [xb.r])

        load(0)
        n = 0
        nev = 0
        for Tq in range(8):
            if Tq + 1 < 8:
                load(Tq + 1)
            xb = xs[Tq % 2]
            h = hT[Tq % 2]
            emit_norm_T(nb, xb, h, 0, 1, 0)
            for hd in range(17):
                col0 = hd * 128 if hd < 16 else 2048
                psq = PS[n % 2]
                pss = PS[2 + n % 2]
                sq = SQ[n % 2]
                rs = RS[n % 2]
                n += 1
                for kc in range(8):
                    P.mm(psq.t[:, :], win.t[:, kc, col0:col0 + 128], h.t[:, kc, :], kc == 0, kc == 7, [win.r, h.r], [psq.r])
                P.act(sq.t[:], psq.t[:, :], AF.Square, [psq.r], [sq.r])
                P.mm(pss.t[:, :], ones, sq.t[:], True, True, [consts.r, sq.r], [pss.r])
                P.act(rs.t[:], pss.t[:, :], AF.Sqrt, [pss.r], [rs.r], bias=EPS, scale=1.0 / 128)
                P.op("dve", lambda e, rs=rs: e.reciprocal(out=rs.t[:], in_=rs.t[:]), [rs.r], [rs.r])
                if hd < 16:
                    P.stt("dve", qnb.t[:, :, hd, :], psq.t[:, :].rearrange("p (j t) -> p j t", j=4), L.gq.t[:, 0:1],
                          rs.t[:].rearrange("p (j t) -> p j t", j=4), ALU.mult, ALU.mult, [psq.r, rs.r, L.gq.r], [qnb.r])
                else:
                    P.stt("dve", L.knT.t[:, Tq * 512:(Tq + 1) * 512], psq.t[:, :], L.gq.t[:, 1:2], rs.t[:], ALU.mult, ALU.mult,
                          [psq.r, rs.r, L.gq.r], [L.knT.r])
            for j in range(4):
                P.dma("sp", qn_s[Tq * 4 + j], qnb.t[:, j, :, :].rearrange("p h t -> p (h t)"), "st_qn", [qnb.r], [Rqn[Tq * 4 + j]])

            def evac(out, in_, reads, writes, scale=None):
                nonlocal nev
                if nev % 2 == 0:
                    if scale is None:
                        P.act(out, in_, AF.Identity, reads, writes)
                    else:
                        P.act(out, in_, AF.Identity, reads, writes, scale=scale)
                else:
                    if scale is None:
                        P.copy("dve", out, in_, reads, writes)
                    else:
                        P.ts("dve", out, in_, scale, None, ALU.mult, None, reads, writes)
                nev += 1

            for j in range(4):
                ps = PS[4]
                for kc in range(8):
                    P.mm(ps.t[:, 0:128], h.t[:, kc, j * 128:(j + 1) * 128], win.t[:, kc, 2048:2176], kc == 0, kc == 7,
                         [win.r, h.r], [ps.r])
                evac(L.vals.t[:, Tq * 4 + j, 0:128], ps.t[:, 0:128], [ps.r], [L.vals.r])
            for c in range(4):
                ps = PS[5]
                for kc in range(8):
                    P.mm(ps.t[:, :], win.t[:, kc, 2176 + c * 128: 2176 + (c + 1) * 128], h.t[:, kc, :], kc == 0, kc == 7,
                         [win.r, h.r], [ps.r])
                evac(L.qiT.t[:, c, Tq * 512:(Tq + 1) * 512], ps.t[:, :], [ps.r], [L.qiT.r])
            ps = PS[4]
            for kc in range(8):
                P.mm(ps.t[:, :], wki2.t[:, kc, :], h.t[:, kc, :], kc == 0, kc == 7, [wki2.r, h.r], [ps.r])
            evac(L.kiT2.t[:, Tq * 512:(Tq + 1) * 512], ps.t[:, :], [ps.r], [L.kiT2.r])
            for j in range(4):
                ps = PS[5]
                for kc in range(8):
                    P.mm(ps.t[:, 0:8], h.t[:, kc, j * 128:(j + 1) * 128], win.t[:, kc, 2752:2760], kc == 0, kc == 7,
                         [win.r, h.r], [ps.r])
                evac(L.wi.t[:, Tq * 4 + j, :], ps.t[:, 0:8], [ps.r], [L.wi.r], scale=8.0 ** -0.5)
        P.barrier()
        P.reset(m)

    def phase_B1(L, qts=range(NT)):
        m = P.mark()
        isc = [T(P.alloc([128, S], F32, "isc")) for _ in range(2)]
        pen = [T(P.alloc([128, S], BF16, "pen")) for _ in range(2)]
        dg = [T(P.alloc([128, 8, 128], BF16, "dg")) for _ in range(2)]
        Rh = [T(P.alloc([128, 512], BF16, "Rh")) for _ in range(8)]
        junk = T(P.alloc([128, S], BF16, "junkc"))
        thr = [T(P.alloc([128, 1], F32, "thr")) for _ in range(2)]
        cand = T(P.alloc([128, 1], F32, "cand"))
        ind = T(P.alloc([128, 1], F32, "ind"))
        cnt = [T(P.alloc([128, IDX_NIT], F32, "cnt")) for _ in range(2)]
        qnq = [T(P.alloc([128, 2048], BF16, "qnq")) for _ in range(2)]
        PTb = [T(P.alloc([128, 512], BF16, "PTb")) for _ in range(3)]
        olat = T(P.alloc([128, 2048], BF16, "olat"))
        olT = T(P.alloc([128, 2048], BF16, "olT"))
        oTq = T(P.alloc([128, 1024], BF16, "oTq"))
        rden = T(P.alloc([128, 16], F32, "rden"))
        xq = [T(P.alloc([128, 1024], F32, "xq")) for _ in range(2)]
        wo = T(P.alloc([128, 8, D], BF16, "wo1"))
        wuvP = T(P.alloc([128, 16, 128], BF16, "wuvP"))
        biasT = T(P.alloc([128, 2, 2048], BF16, "biasT"))
        P.dma("sp", wo.t[:], wo_s[1], "ld_wo", [Rwo[1]], [wo.r])
        P.memset("pool", wuvP.t[:], 0.0, [wuvP.r])
        for hd in range(16):
            c0 = (hd % 2) * 64
            P.dma("pool", wuvP.t[:, hd, c0:c0 + 64], dsa_w_uv[hd], "ld_wuv", (), [wuvP.r])
        P.dma("sp", isc[0].t[:], biasg_in[:, :], "ld_bg", (), [isc[0].r])
        P.dma("sp", isc[1].t[:], biasc_in[:, :], "ld_bc", (), [isc[1].r])
        P.tt("dve", biasT.t[:].rearrange("p w c -> p (w c)"), isc[0].t[:], isc[1].t[:], ALU.subtract, [isc[0].r, isc[1].r], [biasT.r])

        qtl = list(qts)
        ctr = {"z": 0, "ev": 0, "pt": 0}

        def ev2(out, in_, reads, writes, relu=False):
            k = ctr["ev"]
            ctr["ev"] += 1
            if k % 2 == 0:
                P.act(out, in_, AF.Relu if relu else AF.Identity, reads, writes)
            else:
                if relu:
                    P.ts("dve", out, in_, 0.0, None, ALU.max, None, reads, writes)
                else:
                    P.copy("dve", out, in_, reads, writes)

        def idx(qi_):
            qt = qtl[qi_]
            b = qi_ % 2
            nk = (qt + 1) * 128
            P.dma("sp", qnq[b].t[:], qn_s[qt], f"ld_qnq{b}", [Rqn[qt]], [qnq[b].r])
            P.dma("sp", xq[b].t[:], y[qt * 128:(qt + 1) * 128, :], f"ld_xq{b}", [Ry[qt]], [xq[b].r])
            for ih in range(8):
                P.ts("pool" if ih % 2 else "dve", dg[b].t[:, ih, :], ident, L.wi.t[:, qt, ih:ih + 1], None, ALU.mult, None,
                     [consts.r, L.wi.r], [dg[b].r])
            for ck in range((nk + 511) // 512):
                c_lo = ck * 512
                w = min(512, nk - c_lo)
                for ih in range(8):
                    pz = PS[ctr["z"] % 2]
                    ctr["z"] += 1
                    r0 = (ih % 2) * 64
                    P.mm(pz.t[:, 0:w], L.qiT.t[r0:r0 + 64, ih // 2, qt * 128:(qt + 1) * 128], L.kiT2.t[r0:r0 + 64, c_lo:c_lo + w],
                         True, True, [L.qiT.r, L.kiT2.r], [pz.r])
                    ev2(Rh[ih].t[:, 0:w], pz.t[:, 0:w], [pz.r], [Rh[ih].r], relu=True)
                pi = PS[2]
                for ih in range(8):
                    P.mm(pi.t[:, 0:w], dg[b].t[:, ih, :], Rh[ih].t[:, 0:w], ih == 0, ih == 7, [dg[b].r, Rh[ih].r], [pi.r])
                ev2(isc[b].t[:, c_lo:c_lo + w], pi.t[:, 0:w], [pi.r], [isc[b].r])
            P.tt("dve", isc[b].t[:, nk - 128:nk], isc[b].t[:, nk - 128:nk], cmask, ALU.add, [isc[b].r, consts.r], [isc[b].r])
            P.memset("pool", thr[b].t[:], -IDX_LIM, [thr[b].r])
            if qt >= 2:
                P.memset("pool", cnt[b].t[:], 0.0, [cnt[b].r])
                for it in range(IDX_NIT):
                    step = IDX_LIM / (2.0 ** it)
                    P.ts("dve", cand.t[:], thr[b].t[:], step, None, ALU.add, None, [thr[b].r], [cand.r])
                    P.ts("dve", junk.t[:, 0:nk], isc[b].t[:, 0:nk], cand.t[:, 0:1], 0.0, ALU.is_ge, ALU.add,
                         [isc[b].r, cand.r], [junk.r, cnt[b].r], accum_out=cnt[b].t[:, it:it + 1])
                    P.ts("dve", ind.t[:], cnt[b].t[:, it:it + 1], 255.5, step, ALU.is_ge, ALU.mult, [cnt[b].r], [ind.r])
                    P.tt("dve", thr[b].t[:], thr[b].t[:], ind.t[:], ALU.add, [thr[b].r, ind.r], [thr[b].r])
            P.ts("pool", pen[b].t[:, 0:nk], isc[b].t[:, 0:nk], thr[b].t[:, 0:1], -30000.0, ALU.is_lt, ALU.mult,
                 [isc[b].r, thr[b].r], [pen[b].r])

        def att(qi_):
            qt = qtl[qi_]
            b = qi_ % 2
            for hp in range(2):
                touched = set()
                for kb in range(qt + 1):
                    near = kb >= qt - 1
                    for g in range(2):
                        hh0 = hp * 8 + g * 4
                        ps = PS[g]
                        P.mm(ps.t[:, :], L.knT.t[:, kb * 128:(kb + 1) * 128], qnq[b].t[:, hh0 * 128:(hh0 + 4) * 128], True, False,
                             [L.knT.r, qnq[b].r], [ps.r])
                        P.mm(ps.t[:, :], pen[b].t[:, kb * 128:(kb + 1) * 128], irep4, False, not near, [pen[b].r, consts.r], [ps.r])
                        if near:
                            P.mm(ps.t[:, :], ident, biasT.t[:, 0 if kb == qt else 1, hh0 * 128:(hh0 + 4) * 128], False, True,
                                 [consts.r, biasT.r], [ps.r])
                        pt = PTb[ctr["pt"] % 3]
                        ctr["pt"] += 1
                        P.act(pt.t[:], ps.t[:, :], AF.Exp, [ps.r], [pt.r])
                        for i4 in range(4):
                            hd = g * 4 + i4
                            bk = 3 + hd // 3
                            off = (hd % 3) * 129
                            P.mm(PS[bk].t[:, off:off + 129], pt.t[:, i4 * 128:(i4 + 1) * 128], L.vals.t[:, kb, :], bk not in touched, False,
                                 [pt.r, L.vals.r], [PS[bk].r])
                            touched.add(bk)
                for bi in range(3):
                    nh = 3 if bi < 2 else 2
                    P.op("dve", lambda e, bi=bi, nh=nh, hp=hp: e.reciprocal(
                        out=rden.t[:, hp * 8 + bi * 3: hp * 8 + bi * 3 + nh],
                        in_=PS[3 + bi].t[:, 0:nh * 129].rearrange("p (h c) -> p h c", c=129)[:, :, 128]),
                        [PS[3 + bi].r], [rden.r])
                for hd in range(8):
                    bk = 3 + hd // 3
                    off = (hd % 3) * 129
                    gh = hp * 8 + hd
                    if hd % 2 == 0:
                        P.act(olat.t[:, gh * 128:(gh + 1) * 128], PS[bk].t[:, off:off + 128], AF.Identity, [PS[bk].r, rden.r], [olat.r],
                              scale=rden.t[:, gh:gh + 1])
                    else:
                        P.ts("dve", olat.t[:, gh * 128:(gh + 1) * 128], PS[bk].t[:, off:off + 128], rden.t[:, gh:gh + 1], None,
                             ALU.mult, None, [PS[bk].r, rden.r], [olat.r])
            for hd in range(16):
                P.tr(PT[hd // 8].t[:, (hd % 8) * 128:(hd % 8 + 1) * 128], olat.t[:, hd * 128:(hd + 1) * 128], ident,
                     [olat.r, consts.r], [PT[hd // 8].r])
            P.copy("dve", olT.t[:, 0:1024], PT[0].t[:, :], [PT[0].r], [olT.r])
            P.act(olT.t[:, 1024:2048], PT[1].t[:, :], AF.Identity, [PT[1].r], [olT.r])
            for half in range(2):
                bank = PS[2]
                for c4 in range(4):
                    j2 = half * 4 + c4
                    P.mm(bank.t[:, c4 * 128:(c4 + 1) * 128], wuvP.t[:, 2 * j2, :], olT.t[:, (2 * j2) * 128:(2 * j2 + 1) * 128],
                         c4 == 0, False, [wuvP.r, olT.r], [bank.r])
                    P.mm(bank.t[:, c4 * 128:(c4 + 1) * 128], wuvP.t[:, 2 * j2 + 1, :], olT.t[:, (2 * j2 + 1) * 128:(2 * j2 + 2) * 128],
                         False, True, [wuvP.r, olT.r], [bank.r])
                ev2(oTq.t[:, half * 512:(half + 1) * 512], bank.t[:, :], [bank.r], [oTq.r])
            for half in range(2):
                bank = PS[half]
                for c in range(8):
                    P.mm(bank.t[:, :], oTq.t[:, c * 128:(c + 1) * 128], wo.t[:, c, half * 512:(half + 1) * 512], c == 0, c == 7,
                         [oTq.r, wo.r], [bank.r])
                P.tt("dve", xq[b].t[:, half * 512:(half + 1) * 512], bank.t[:, :], xq[b].t[:, half * 512:(half + 1) * 512], ALU.add,
                     [bank.r, xq[b].r], [xq[b].r])
            P.dma("sp", y[qt * 128:(qt + 1) * 128, :], xq[b].t[:], f"st_xq{b}", [xq[b].r], [Ry[qt]])

        idx(0)
        for qi_ in range(len(qtl)):
            if qi_ + 1 < len(qtl):
                idx(qi_ + 1)
            att(qi_)
        P.barrier()
        P.reset(m)

    order = ["P", "A0", "B0", "C0", "F0", "A1", "B1", "F1"]
    upto = order.index(stop) if stop is not None else len(order) - 1
    if n_layers == 1:
        upto = min(upto, order.index("F0"))
    phase_P()
    if dbg:
        dbg_mod = nc.dram_tensor("dbg_mod", [128, 128], F32, kind="ExternalOutput").ap()
        P.dma("sp", dbg_mod[:, 0:96], modfm.t[:], "st_dbg0", [modfm.r], [])
        P.dma("sp", dbg_mod[:, 96:128], gs.t[:], "st_dbg1", [gs.r], [])
    P.barrier()
    P.reset(pers_mark)
    if upto >= 1:
        phase_A0()
    if upto >= 2:
        phase_B0()
    if upto >= 3:
        phase_C(0)
    if upto >= 4:
        phase_F(0)
    if upto >= 5:
        L = alloc_L1()
        phase_A1(L)
    if upto >= 6:
        phase_B1(L)
        P.reset(pers_mark)
    if upto >= 7:
        phase_F(1)
    P.barrier()
    P.emit()
    return nc, P


def make_consts():
    c = np.zeros((128, NCONST), np.float32)
    p = np.arange(128)[:, None]
    q = np.arange(128)[None, :]
    c[:, C_ID:C_ID + 128] = (p == q)
    c[:, C_TRI:C_TRI + 128] = (p < q)
    c[:, C_NEGM:C_NEGM + 128] = np.where(p >= q, -30000.0, 0.0)
    c[:, C_NEGU:C_NEGU + 128] = np.where(p >= q, -1.0, 0.0)
    c[:, C_ONES:C_ONES + 128] = 1.0
    c[:, C_NEGI:C_NEGI + 128] = -1.0 * (p == q)
    c[:, C_CMASK:C_CMASK + 128] = np.where(q > p, -1e30, 0.0)
    for r_ in range(4):
        c[:, C_IREP + r_ * 128:C_IREP + (r_ + 1) * 128] = (p == q)
    return c.astype(ml_dtypes.bfloat16)


def make_consts2():
    osel = np.zeros((128, 32, 128), np.float32)
    for i in range(32):
        osel[:, i, i] = 1.0
    ltm = np.zeros((128, 96), np.float32)
    k = np.arange(32)[:, None]
    i_ = np.arange(32)[None, :]
    ltm[0:32, 64:96] = (k > i_)
    nsel = np.zeros((32, 32, 128), np.float32)
    for i in range(32):
        nsel[i, i, :] = -1.0
    bf = ml_dtypes.bfloat16
    return osel.reshape(128, 4096).astype(bf), ltm.astype(bf), nsel.reshape(32, 4096).astype(bf)


def t5_bucket_np(dist):
    n = np.maximum(dist, 0)
    nf = np.maximum(n, 1).astype(np.float32)
    large = 16 + (np.log(nf / 16) / np.float32(math.log(128 / 16)) * 16).astype(np.int32)
    large = np.minimum(large, 31)
    return np.where(n < 16, n, large)


def prep_inputs(inp, cores):
    f = lambda a: np.ascontiguousarray(np.asarray(a, dtype=np.float32))
    x = f(inp["x"])
    c = f(inp["c"])
    ada_w = f(inp["ada_w"])
    ada_b = f(inp["ada_b"])
    ada_bT = np.ascontiguousarray(ada_b.reshape(2, 48, 128).transpose(2, 0, 1).reshape(128, 96))
    nmixT = np.ascontiguousarray(f(inp["norm_mix"]).reshape(2, 8, 128).transpose(2, 0, 1).reshape(128, 16))
    nffnT = np.ascontiguousarray(f(inp["norm_ffn"]).reshape(2, 8, 128).transpose(2, 0, 1).reshape(128, 16))
    cw = f(inp["ffn_conv_w"])
    cb = f(inp["ffn_conv_b"])
    conv = np.concatenate([cw, cb[:, None, :]], axis=1)
    convT = np.ascontiguousarray(conv.reshape(2, 4, NCH, 128).transpose(3, 0, 2, 1).reshape(128, 2 * NCH * 4))
    rb = f(inp["rel_bias"])
    p_ = np.arange(128)[:, None]
    q_ = np.arange(128)[None, :]
    bg = np.zeros((128, 2, 16, 128), np.float32)
    for w_ in range(2):
        bk = t5_bucket_np(w_ * 128 + q_ - p_)
        bg[:, w_, :, :] = rb[bk].transpose(0, 2, 1)
    bc = np.broadcast_to(rb[31][None, None, :, None], (128, 2, 16, 128))
    osel_c, sm_c, nsel_c = make_consts2()
    shared = {
        "osel": osel_c, "smc": sm_c, "nsel": nsel_c,
        "ada_w": ada_w, "ada_b": ada_b, "ada_bT": ada_bT, "nmixT": nmixT, "nffnT": nffnT,
        "sb_w_in": f(inp["sb_w_in"])[0], "sb_w_out": f(inp["sb_w_out"])[0], "dsa_w_out": f(inp["dsa_w_out"])[0],
        "ffn_w_up": f(inp["ffn_w_up"]), "ffn_w_down": f(inp["ffn_w_down"]),
        "convT": convT, "consts": make_consts(),
        "dsa_w_in": f(inp["dsa_w_in"])[0], "dsa_w_uv": f(inp["dsa_w_uv"])[0],
        "dsa_gT": np.ascontiguousarray(np.stack([f(inp["dsa_q_norm"])[0], f(inp["dsa_k_norm"])[0]], axis=1)),
        "biasg": np.ascontiguousarray(bg.reshape(128, 4096)), "biasc": np.ascontiguousarray(bc.reshape(128, 4096)),
    }
    maps = []
    for b in cores:
        d = dict(shared)
        d["x"] = x[b]
        d["cT"] = np.ascontiguousarray(c[b].reshape(8, 128).T)
        maps.append(d)
    return maps


_CACHE = {}


def kernel(**inputs):
    if "nc" not in _CACHE:
        _CACHE["nc"] = build()[0]
    nc = _CACHE["nc"]
    maps = prep_inputs(inputs, range(8))
    res = run_bass_kernel_spmd(nc, maps, core_ids=list(range(8)))
    out = np.stack([np.asarray(r["y"]) for r in res.results], axis=0)
    return out.astype(np.float32)
```

```python
import math
from contextlib import ExitStack

import numpy as np
import ml_dtypes

import concourse.bass as bass
import concourse.mybir as mybir
from concourse.bass_utils import run_bass_kernel_spmd

F32 = mybir.dt.float32
BF16 = mybir.dt.bfloat16
ALU = mybir.AluOpType
AF = mybir.ActivationFunctionType

S = 4096
D = 1024
NT = S // 128
FFN = 2816
NCH = 2 * FFN // 128
EPS = 1e-6
SB_BASE = 16512
SB_TOP = 229344

C_ID, C_TRI, C_NEGM, C_NEGU, C_ONES, C_NEGI, C_CMASK, C_IREP = 0, 128, 256, 384, 512, 640, 768, 896
NCONST = 896 + 512
IDX_LIM = 128.0
IDX_NIT = 20


class Region:
    __slots__ = ("w", "r")

    def __init__(self):
        self.w = None
        self.r = {}


class Prog:
    ENG = ("pe", "act", "dve", "pool", "sp")

    def __init__(self, nc):
        self.nc = nc
        self.streams = {e: [] for e in self.ENG}
        self.count = {}
        self.waited = {e: {} for e in self.ENG}
        self.cur = SB_BASE
        self.nalloc = 0
        self.nwaits = 0

    def alloc(self, shape, dtype, name="t"):
        nbytes = int(np.prod(shape[1:])) * (4 if dtype == F32 else 2)
        nbytes = (nbytes + 63) // 64 * 64
        off = self.cur
        assert off + nbytes <= SB_TOP, f"SBUF overflow allocating {name} {shape}"
        self.cur += nbytes
        self.nalloc += 1
        return self.nc.alloc_sbuf_tensor_at(f"{name}_{self.nalloc}", list(shape), dtype, offset=off)

    def mark(self):
        return self.cur

    def reset(self, m):
        self.cur = m

    def _need(self, eng, waits, tok):
        if tok is None:
            return
        k, v = tok
        if self.waited[eng].get(k, 0) >= v:
            return
        if waits.get(k, 0) < v:
            waits[k] = v

    def _deps(self, eng, reads, writes):
        waits = {}
        for r in reads:
            if r.w is not None:
                if not (eng == "pe" and r.w[0] == "pe"):
                    self._need(eng, waits, r.w)
        for w in writes:
            if w.w is not None and w.w[0] != eng:
                self._need(eng, waits, w.w)
            for k, v in w.r.items():
                if k != eng:
                    self._need(eng, waits, (k, v))
        for k, v in waits.items():
            self.waited[eng][k] = v
        return list(waits.items())

    def _commit(self, tok, reads, writes):
        k, v = tok
        for r in reads:
            if r.r.get(k, 0) < v:
                r.r[k] = v
        for w in writes:
            w.w = tok
            w.r = {}

    def op(self, eng, fn, reads=(), writes=()):
        waits = self._deps(eng, reads, writes)
        self.count[eng] = self.count.get(eng, 0) + 1
        tok = (eng, self.count[eng])
        self.streams[eng].append((waits, fn, eng, 1))
        self.nwaits += len(waits)
        self._commit(tok, reads, writes)

    def dma(self, q, out, in_, sem, reads=(), writes=()):
        waits = self._deps(q, reads, writes)
        self.count[sem] = self.count.get(sem, 0) + 16
        tok = (sem, self.count[sem])
        self.streams[q].append((waits, lambda e: e.dma_start(out=out, in_=in_), sem, 16))
        self.nwaits += len(waits)
        self._commit(tok, reads, writes)

    def barrier(self):
        snap = dict(self.count)
        for e in self.ENG:
            waits = []
            for k, v in snap.items():
                if v > 0 and self.waited[e].get(k, 0) < v:
                    waits.append((k, v))
                    self.waited[e][k] = v
            if waits:
                self.streams[e].append((waits, None, None, 0))

    def emit(self):
        nc = self.nc
        with ExitStack() as es:
            sems = {}
            for k in self.count:
                sems[k] = es.enter_context(nc.semaphore("s_" + k))
            block = es.enter_context(nc.Block())

            def run(stream):
                def f(e):
                    for waits, fn, semk, inc in stream:
                        for k, v in waits:
                            e.wait_ge(sems[k], v)
                        if fn is not None:
                            fn(e).then_inc(sems[semk], inc)
                return f

            block.tensor(run(self.streams["pe"]))
            block.scalar(run(self.streams["act"]))
            block.vector(run(self.streams["dve"]))
            block.gpsimd(run(self.streams["pool"]))
            block.sync(run(self.streams["sp"]))

    def mm(self, out, lhsT, rhs, start, stop, reads, writes):
        self.op("pe", lambda e: e.matmul(out, lhsT, rhs, start=start, stop=stop), reads, writes)

    def tr(self, out, in_, ident, reads, writes):
        self.op("pe", lambda e: e.transpose(out, in_, ident), reads, writes)

    def act(self, out, in_, func, reads, writes, bias=None, scale=None, accum_out=None):
        kw = {}
        if bias is not None:
            kw["bias"] = bias
        if scale is not None:
            kw["scale"] = scale
        if accum_out is not None:
            kw["accum_out"] = accum_out
        self.op("act", lambda e: e.activation(out=out, in_=in_, func=func, **kw), reads, writes)

    def ts(self, eng, out, in0, s1, s2, op0, op1, reads, writes, accum_out=None):
        kw = {}
        if accum_out is not None:
            kw["accum_out"] = accum_out
        if op1 is None:
            self.op(eng, lambda e: e.tensor_scalar(out=out, in0=in0, scalar1=s1, scalar2=None, op0=op0, **kw), reads, writes)
        else:
            self.op(eng, lambda e: e.tensor_scalar(out=out, in0=in0, scalar1=s1, scalar2=s2, op0=op0, op1=op1, **kw), reads, writes)

    def tt(self, eng, out, in0, in1, op, reads, writes):
        self.op(eng, lambda e: e.tensor_tensor(out=out, in0=in0, in1=in1, op=op), reads, writes)

    def stt(self, eng, out, in0, scalar, in1, op0, op1, reads, writes):
        self.op(eng, lambda e: e.scalar_tensor_tensor(out=out, in0=in0, scalar=scalar, in1=in1, op0=op0, op1=op1), reads, writes)

    def copy(self, eng, out, in_, reads, writes):
        self.op(eng, lambda e: e.tensor_copy(out=out, in_=in_), reads, writes)

    def memset(self, eng, ap, val, writes):
        self.op(eng, lambda e: e.memset(ap, val), (), writes)


class T:
    def __init__(self, t, n=1):
        self.t = t
        self.R = [Region() for _ in range(n)]
        self.r = self.R[0]


def build(n_layers=2, dbg=False, stop=None):
    nc = bass.Bass("TRN2", target_bir_lowering=False)
    P = Prog(nc)

    def din(name, shape, dt=F32):
        return nc.dram_tensor(name, list(shape), dt, kind="ExternalInput").ap()

    def dscr(name, shape, dt=BF16):
        kind = "ExternalOutput" if dbg else "Internal"
        return nc.dram_tensor(name, list(shape), dt, kind=kind).ap()

    x_in = din("x", [S, D])
    cT_in = din("cT", [128, 8])
    ada_w = din("ada_w", [2, D, 6 * D])
    ada_b = din("ada_b", [2, 6 * D])
    ada_bT = din("ada_bT", [128, 96])
    nmixT = din("nmixT", [128, 16])
    nffnT = din("nffnT", [128, 16])
    sb_w_in = din("sb_w_in", [D, 3 * D])
    sb_w_out = din("sb_w_out", [D, D])
    dsa_w_out_in = din("dsa_w_out", [D, D])
    ffn_w_up = din("ffn_w_up", [2, D, 2 * FFN])
    ffn_w_down = din("ffn_w_down", [2, FFN, D])
    convT = din("convT", [128, 2 * NCH * 4])
    consts_in = din("consts", [128, NCONST], BF16)
    dsa_w_in = din("dsa_w_in", [D, 2760])
    dsa_gT = din("dsa_gT", [128, 2])
    dsa_w_uv = din("dsa_w_uv", [16, 128, 64])
    biasg_in = din("biasg", [128, 4096])
    biasc_in = din("biasc", [128, 4096])
    osel_in = din("osel", [128, 4096], BF16)
    sm_in = din("smc", [128, 96], BF16)
    nsel_in = din("nsel", [32, 4096], BF16)
    y = nc.dram_tensor("y", [S, D], F32, kind="ExternalOutput").ap()

    qkT_s = dscr("qkT_s", [16, 128, S])
    v_s = dscr("v_s", [S, D])
    oT_s = dscr("oT_s", [D, S])
    qn_s = dscr("qn_s", [NT, 128, 2048])
    wo_s = dscr("wo_s", [2, 128, 8, D])
    wd_s = dscr("wd_s", [2, 128, 22, D])

    Ry = [Region() for _ in range(NT)]
    Rqk = [[Region() for _ in range(8)] for _ in range(16)]
    Rv = [Region() for _ in range(NT)]
    Ro = [[Region() for _ in range(8)] for _ in range(16)]
    Rqn = [Region() for _ in range(NT)]
    Rwo = [Region(), Region()]
    Rwd = [Region(), Region()]

    PS = [T(nc.alloc_psum_tensor(f"ps{i}", [128, 512], F32)) for i in range(6)]
    PT = [T(nc.alloc_psum_tensor(f"pt{i}", [128, 1024], BF16)) for i in range(2)]

    consts = T(P.alloc([128, NCONST], BF16, "consts"))
    cst = consts.t
    ident = cst[:, C_ID:C_ID + 128]
    tri = cst[:, C_TRI:C_TRI + 128]
    negm = cst[:, C_NEGM:C_NEGM + 128]
    negU = cst[:, C_NEGU:C_NEGU + 128]
    ones = cst[:, C_ONES:C_ONES + 128]
    negI = cst[:, C_NEGI:C_NEGI + 128]
    cmask = cst[:, C_CMASK:C_CMASK + 128]
    irep4 = cst[:, C_IREP:C_IREP + 512]
    modfm = T(P.alloc([128, 96], F32, "modfm"))
    gs = T(P.alloc([128, 32], F32, "gs"))
    convw = T(P.alloc([128, 2 * NCH * 4], F32, "convw"))
    P.dma("sp", consts.t[:], consts_in[:, :], "ld_c0", (), [consts.r])
    P.dma("sp", convw.t[:], convT[:, :], "ld_c1", (), [convw.r])
    pers_mark = P.mark()

    def sh_ap(layer, which, c):
        col = layer * 48 + (0 if which == 0 else 24) + c
        return modfm.t[:, col:col + 1]

    def gs_ap(layer, which, c):
        col = layer * 16 + which * 8 + c
        return gs.t[:, col:col + 1]

    def phase_P():
        cT = T(P.alloc([128, 8], F32, "cT"))
        cond = T(P.alloc([128, 8], F32, "cond"))
        condb = T(P.alloc([128, 8], BF16, "condb"))
        crep = T(P.alloc([128, 8, 128], BF16, "crep"))
        abT = T(P.alloc([128, 96], F32, "abT"))
        nmx = T(P.alloc([128, 16], F32, "nmx"))
        nff = T(P.alloc([128, 16], F32, "nff"))
        P.dma("sp", cT.t[:], cT_in[:, :], "ld_p0", (), [cT.r])
        P.dma("sp", abT.t[:], ada_bT[:, :], "ld_p1", (), [abT.r])
        P.dma("sp", nmx.t[:], nmixT[:, :], "ld_p2", (), [nmx.r])
        P.dma("sp", nff.t[:], nffnT[:, :], "ld_p3", (), [nff.r])
        P.act(cond.t[:], cT.t[:], AF.Silu, [cT.r], [cond.r])
        P.copy("dve", condb.t[:], cond.t[:], [cond.r], [condb.r])
        for kc in range(8):
            P.ts("dve", crep.t[:, kc, :], ones, cond.t[:, kc:kc + 1], None, ALU.mult, None,
                 [consts.r, cond.r], [crep.r])
        wp = [T(P.alloc([128, 8, 1024], BF16, "adawp")) for _ in range(2)]
        abb = [T(P.alloc([128, 1024], F32, "abb")) for _ in range(2)]
        gbc = [T(P.alloc([128, 1024], F32, "gbc")) for _ in range(2)]
        wst = [T(P.alloc([128, 1024], F32, "wst")) for _ in range(3)]
        wob = [T(P.alloc([128, 1024], BF16, "wob")) for _ in range(3)]
        PM = PS[0]
        npiece = 0
        nst = 0
        for layer in range(n_layers):
            for g in range(6):
                w = wp[npiece % 2]
                for kc in range(8):
                    P.dma("pool", w.t[:, kc, :], ada_w[layer, kc * 128:(kc + 1) * 128, g * 1024:(g + 1) * 1024],
                          f"ld_aw{npiece % 2}", (), [w.r])
                npiece += 1
                if g in (0, 1, 3, 4):
                    for oc in range(8):
                        col = layer * 48 + g * 8 + oc
                        for kc in range(8):
                            P.mm(PM.t[:, col:col + 1], w.t[:, kc, oc * 128:(oc + 1) * 128], condb.t[:, kc:kc + 1],
                                 kc == 0, kc == 7, [w.r, condb.r], [PM.r])
                else:
                    which = 0 if g == 2 else 1
                    ab = abb[which]
                    P.dma("sp", ab.t[:], ada_b[layer:layer + 1, g * 1024:(g + 1) * 1024].partition_broadcast(128),
                          f"ld_abb{which}", (), [ab.r])
                    gb = gbc[which]
                    for half in range(2):
                        PG = PS[1 + half]
                        for kc in range(8):
                            P.mm(PG.t[:, :], crep.t[:, kc, :], w.t[:, kc, half * 512:(half + 1) * 512],
                                 kc == 0, kc == 7, [w.r, crep.r], [PG.r])
                        P.tt("dve", gb.t[:, half * 512:(half + 1) * 512], PG.t[:, :], ab.t[:, half * 512:(half + 1) * 512],
                             ALU.add, [PG.r, ab.r], [gb.r])
                    if which == 0:
                        srcs = [(sb_w_out if layer == 0 else dsa_w_out_in)[c * 128:(c + 1) * 128, :] for c in range(8)]
                        dsts = [wo_s[layer, :, c, :] for c in range(8)]
                        Rdst = Rwo[layer]
                    else:
                        srcs = [ffn_w_down[layer, c * 128:(c + 1) * 128, :] for c in range(22)]
                        dsts = [wd_s[layer, :, c, :] for c in range(22)]
                        Rdst = Rwd[layer]
                    for sa, da in zip(srcs, dsts):
                        ws = wst[nst % 3]
                        wb = wob[nst % 3]
                        P.dma("sp", ws.t[:], sa, f"ld_wst{nst % 3}", (), [ws.r])
                        P.tt("dve" if nst % 2 == 0 else "pool", wb.t[:], ws.t[:], gb.t[:], ALU.mult, [ws.r, gb.r], [wb.r])
                        P.dma("sp", da, wb.t[:], f"st_wob{nst % 3}", [wb.r], [Rdst])
                        nst += 1
        ncol = 48 * n_layers
        P.tt("dve", modfm.t[:, 0:ncol], PM.t[:, 0:ncol], abT.t[:, 0:ncol], ALU.add, [PM.r, abT.r], [modfm.r])
        for layer in range(n_layers):
            for which in range(2):
                sc0 = layer * 48 + (8 if which == 0 else 32)
                nrm = (nmx if which == 0 else nff).t[:, layer * 8:(layer + 1) * 8]
                P.stt("dve", gs.t[:, layer * 16 + which * 8: layer * 16 + which * 8 + 8],
                      modfm.t[:, sc0:sc0 + 8], 1.0, nrm, ALU.add, ALU.mult, [modfm.r, nmx.r, nff.r], [gs.r])


    class NormBufs:
        def __init__(self, ntt):
            self.ntt = ntt
            self.junk = T(P.alloc([128, 1024], BF16, "junk"))
            self.ss = T(P.alloc([128, 4], F32, "ss"))
            self.rs = T(P.alloc([128, 4], F32, "rs"))
            self.rstd = T(P.alloc([128, 4], F32, "rstd"))
            self.xn = T(P.alloc([128, ntt, 1024], BF16, "xn"))
            self.n = 0

    def emit_norm_T(nb, xs, hT, hoff, layer, which):
        ntt = nb.ntt
        P.memset("pool", nb.ss.t[:], 0.0, [nb.ss.r])
        for j in range(ntt):
            P.act(nb.junk.t[:], xs.t[:, j, :], AF.Square, [xs.r], [nb.junk.r, nb.ss.r], accum_out=nb.ss.t[:, j:j + 1])
        P.act(nb.rs.t[:, 0:ntt], nb.ss.t[:, 0:ntt], AF.Sqrt, [nb.ss.r], [nb.rs.r], bias=EPS, scale=1.0 / D)
        P.op("dve", lambda e: e.reciprocal(out=nb.rstd.t[:, 0:ntt], in_=nb.rs.t[:, 0:ntt]), [nb.rs.r], [nb.rstd.r])
        for j in range(ntt):
            P.ts("dve" if j % 2 == 0 else "pool", nb.xn.t[:, j, :], xs.t[:, j, :], nb.rstd.t[:, j:j + 1], None,
                 ALU.mult, None, [xs.r, nb.rstd.r], [nb.xn.r])
        W = ntt * 128
        per = 1024 // W
        c = 0
        while c < 8:
            pt = PT[nb.n % 2]
            nb.n += 1
            for cc in range(per):
                for j in range(ntt):
                    P.tr(pt.t[:, cc * W + j * 128: cc * W + (j + 1) * 128], nb.xn.t[:, j, (c + cc) * 128:(c + cc + 1) * 128],
                         ident, [nb.xn.r, consts.r], [pt.r])
            for cc in range(per):
                ch = c + cc
                if ch % 2 == 0:
                    P.act(hT.t[:, ch, hoff:hoff + W], pt.t[:, cc * W:(cc + 1) * W], AF.Identity, [pt.r, gs.r, modfm.r], [hT.r],
                          bias=sh_ap(layer, which, ch), scale=gs_ap(layer, which, ch))
                else:
                    P.ts("dve", hT.t[:, ch, hoff:hoff + W], pt.t[:, cc * W:(cc + 1) * W], gs_ap(layer, which, ch),
                         sh_ap(layer, which, ch), ALU.mult, ALU.add, [pt.r, gs.r, modfm.r], [hT.r])
            c += per

    def x_src(layer):
        return x_in if layer == 0 else y

    def phase_A0():
        m = P.mark()
        win = T(P.alloc([128, 8, 3 * D], BF16, "win"))
        for kc in range(8):
            P.dma("pool", win.t[:, kc, :], sb_w_in[kc * 128:(kc + 1) * 128, :], "ld_win", (), [win.r])
        xs = [T(P.alloc([128, 4, 1024], F32, "xs")) for _ in range(2)]
        nb = NormBufs(4)
        hT = [T(P.alloc([128, 8, 512], BF16, "hT")) for _ in range(2)]
        qk = [T(P.alloc([128, 512], BF16, "qk")) for _ in range(4)]
        vs = [T(P.alloc([128, 1024], BF16, "vs")) for _ in range(2)]

        def load(Tq):
            xb = xs[Tq % 2]
            P.dma("sp", xb.t[:], x_in[Tq * 512:(Tq + 1) * 512, :].rearrange("(j p) d -> p j d", p=128),
                  f"ld_xs{Tq % 2}", [], [xb.r])

        load(0)
        nev = 0
        for Tq in range(8):
            if Tq + 1 < 8:
                load(Tq + 1)
            xb = xs[Tq % 2]
            h = hT[Tq % 2]
            emit_norm_T(nb, xb, h, 0, 0, 0)
            for oc in range(16):
                ps = PS[oc % 4]
                for kc in range(8):
                    P.mm(ps.t[:, :], win.t[:, kc, oc * 128:(oc + 1) * 128], h.t[:, kc, :], kc == 0, kc == 7,
                         [win.r, h.r], [ps.r])
                qb = qk[oc % 4]
                sc = 0.125 if oc < 8 else 1.0
                if nev % 2 == 0:
                    P.act(qb.t[:], ps.t[:, :], AF.Identity, [ps.r], [qb.r], scale=sc)
                else:
                    P.ts("dve", qb.t[:], ps.t[:, :], sc, None, ALU.mult, None, [ps.r], [qb.r])
                nev += 1
                P.dma("sp", qkT_s[oc, :, Tq * 512:(Tq + 1) * 512], qb.t[:], f"st_qk{oc % 4}", [qb.r],
                      [Rqk[oc][Tq]])
            for j in range(4):
                vb = vs[j % 2]
                for half in range(2):
                    ps = PS[4 + half]
                    for kc in range(8):
                        P.mm(ps.t[:, :], h.t[:, kc, j * 128:(j + 1) * 128], win.t[:, kc, 2048 + half * 512: 2048 + (half + 1) * 512],
                             kc == 0, kc == 7, [win.r, h.r], [ps.r])
                    if nev % 2 == 0:
                        P.act(vb.t[:, half * 512:(half + 1) * 512], ps.t[:, :], AF.Identity, [ps.r], [vb.r])
                    else:
                        P.copy("dve", vb.t[:, half * 512:(half + 1) * 512], ps.t[:, :], [ps.r], [vb.r])
                    nev += 1
                tile = Tq * 4 + j
                P.dma("sp", v_s[tile * 128:(tile + 1) * 128, :], vb.t[:], f"st_v{j % 2}", [vb.r], [Rv[tile]])
        P.barrier()
        P.reset(m)

    def phase_B0(heads=range(16), Qs=range(8)):
        m = P.mark()
        kT = [T(P.alloc([96, S], BF16, "kT")) for _ in range(2)]
        vh = [T(P.alloc([128, NT, 64], BF16, "vh")) for _ in range(2)]
        qT = [T(P.alloc([96, 512], BF16, "qT"), n=2) for _ in range(3)]
        u = [T(P.alloc([128, 512], F32, "u")) for _ in range(2)]
        spall = [T(P.alloc([128, NT, 512], BF16, "spall"), n=NT) for _ in range(2)]
        A = [T(P.alloc([128, 512], BF16, "A")) for _ in range(3)]
        osb = [T(P.alloc([64, 512], BF16, "osb")) for _ in range(2)]
        csb = [T(P.alloc([128, 512], BF16, "csb")) for _ in range(2)]
        osel = T(P.alloc([128, 32 * 128], BF16, "osel"))
        ltm = T(P.alloc([128, 96], BF16, "ltm"))
        P.dma("sp", osel.t[:], osel_in[:, :], "ld_osel", (), [osel.r])
        P.dma("sp", ltm.t[:], sm_in[:, :], "ld_sm", (), [ltm.r])
        for i_ in range(2):
            P.dma("sp", kT[i_].t[64:96, :], nsel_in[:, :], f"ld_ns{i_}", (), [kT[i_].r])
        PZ = [PS[0], PS[1]]
        PC = [PS[2], PS[3]]
        PL = PS[4]
        PO = PS[5]
        work = [(h, Q) for h in heads for Q in Qs]
        hlist = list(heads)

        def load_head(hi, h):
            kb_ = kT[hi % 2]
            vb_ = vh[hi % 2]
            oc = 8 + h // 2
            r0 = (h % 2) * 64
            P.dma("sp", kb_.t[0:64, :], qkT_s[oc, r0:r0 + 64, :], f"ld_kT{hi % 2}", [Rqk[oc][t_] for t_ in range(8)], [kb_.r])
            for q4 in range(4):
                P.dma("sp", vb_.t[:, q4 * 8:(q4 + 1) * 8, :],
                      v_s[q4 * 1024:(q4 + 1) * 1024, h * 64:(h + 1) * 64].rearrange("(k p) d -> p k d", p=128),
                      f"ld_vh{hi % 2}", Rv, [vb_.r])

        def load_q(wi):
            h, Q = work[wi]
            qb = qT[wi % 3]
            oc = h // 2
            r0 = (h % 2) * 64
            P.dma("sp", qb.t[0:64, :], qkT_s[oc, r0:r0 + 64, Q * 512:(Q + 1) * 512], f"ld_qT{wi % 3}", [Rqk[oc][Q]],
                  [qb.R[0], qb.R[1]])

        cz = [0]
        ca = [0]

        def geom(Q, i):
            kb = 4 * Q + 3 - i
            j = kb - 4 * Q
            c0 = max(j, 0) * 128
            return kb, j, c0

        def pass1(wi):
            h, Q = work[wi]
            hi = hlist.index(h)
            kb_ = kT[hi % 2]
            qb = qT[wi % 3]
            spb = spall[wi % 2]
            n = 4 * Q + 4

            def zmm(i):
                kb, j, c0 = geom(Q, i)
                pz = PZ[(cz[0] + i) % 2]
                P.mm(pz.t[:, c0:512], kb_.t[0:64, kb * 128:(kb + 1) * 128], qb.t[0:64, c0:512], True, True, [kb_.r, qb.R[0]], [pz.r])

            def rest(i):
                kb, j, c0 = geom(Q, i)
                bz = (cz[0] + i) % 2
                pz = PZ[bz]
                P.act(u[bz].t[:, c0:512], pz.t[:, c0:512], AF.Exp, [pz.r], [u[bz].r])
                P.act(spb.t[:, i, c0:512], u[bz].t[:, c0:512], AF.Ln, [u[bz].r], [spb.R[i]], bias=1.0)
                if j >= 0:
                    P.tt("dve", spb.t[:, i, c0:c0 + 128], spb.t[:, i, c0:c0 + 128], tri, ALU.mult, [spb.R[i], consts.r], [spb.R[i]])
                P.mm(PL.t[:, c0:512], osel.t[:, kb * 128:(kb + 1) * 128], spb.t[:, i, c0:512], i == 0, i == n - 1,
                     [osel.r, spb.R[i]], [PL.r])

            zmm(0)
            for i in range(n):
                if i + 1 < n:
                    zmm(i + 1)
                rest(i)
            cz[0] += n
            cb = csb[wi % 2]
            P.copy("dve", cb.t[:], PL.t[:, :], [PL.r], [cb.r])
            P.mm(PL.t[0:96, :], ltm.t[:, :], cb.t[:], True, True, [ltm.r, cb.r], [PL.r])
            P.copy("dve", qb.t[64:96, :], PL.t[64:96, :], [PL.r], [qb.R[1]])

        def pass2(wi):
            h, Q = work[wi]
            hi = hlist.index(h)
            kb_ = kT[hi % 2]
            vb_ = vh[hi % 2]
            qb = qT[wi % 3]
            spb = spall[wi % 2]
            n = 4 * Q + 4

            def cmm(i):
                kb, j, c0 = geom(Q, i)
                pc = PC[(ca[0] + i) % 2]
                P.mm(pc.t[:, c0:512], kb_.t[:, kb * 128:(kb + 1) * 128], qb.t[:, c0:512], True, False, [kb_.r, qb.R[0], qb.R[1]], [pc.r])
                P.mm(pc.t[:, c0:512], negU, spb.t[:, i, c0:512], False, not (j >= 0), [consts.r, spb.R[i]], [pc.r])
                if j >= 0:
                    P.mm(pc.t[:, c0:c0 + 128], ident, negm, False, True, [consts.r], [pc.r])

            def rest(i):
                kb, j, c0 = geom(Q, i)
                pc = PC[(ca[0] + i) % 2]
                ab = A[(ca[0] + i) % 3]
                P.act(ab.t[:, c0:512], pc.t[:, c0:512], AF.Exp, [pc.r], [ab.r])
                P.mm(PO.t[0:64, c0:512], vb_.t[:, kb, :], ab.t[:, c0:512], i == 0, i == n - 1, [vb_.r, ab.r], [PO.r])

            cmm(0)
            for i in range(n):
                if i + 1 < n:
                    cmm(i + 1)
                rest(i)
            ca[0] += n
            ob = osb[wi % 2]
            P.copy("dve", ob.t[:], PO.t[0:64, :], [PO.r], [ob.r])
            P.dma("sp", oT_s[h * 64:(h + 1) * 64, Q * 512:(Q + 1) * 512], ob.t[:], f"st_o{wi % 2}", [ob.r], [Ro[h][Q]])

        load_head(0, hlist[0])
        load_q(0)
        nW = len(work)
        for wi in range(nW + 1):
            if wi < nW:
                if wi + 1 < nW:
                    load_q(wi + 1)
                pass1(wi)
            if wi >= 1:
                pass2(wi - 1)
            if wi < nW:
                h, Q = work[wi]
                hi = hlist.index(h)
                if Q == list(Qs)[0] and hi + 1 < len(hlist):
                    load_head(hi + 1, hlist[hi + 1])
        P.barrier()
        P.reset(m)

    def phase_C(layer):
        m = P.mark()
        wo = T(P.alloc([128, 8, D], BF16, "wo"))
        P.dma("sp", wo.t[:], wo_s[layer], "ld_wo", [Rwo[layer]], [wo.r])
        ob = [T(P.alloc([128, 8, 512], BF16, "ob")) for _ in range(2)]
        xs = [T(P.alloc([128, 4, 1024], F32, "xs")) for _ in range(2)]
        src = x_src(layer)

        def load(Tq):
            o_ = ob[Tq % 2]
            x_ = xs[Tq % 2]
            P.dma("sp", o_.t[:], oT_s[:, Tq * 512:(Tq + 1) * 512].rearrange("(c p) t -> p c t", p=128), f"ld_ob{Tq % 2}",
                  [Ro[h][Tq] for h in range(16)], [o_.r])
            P.dma("sp", x_.t[:], src[Tq * 512:(Tq + 1) * 512, :].rearrange("(j p) d -> p j d", p=128), f"ld_xs{Tq % 2}",
                  [Ry[Tq * 4 + j] for j in range(4)] if layer > 0 else [], [x_.r])

        load(0)
        nps = 0
        for Tq in range(8):
            if Tq + 1 < 8:
                load(Tq + 1)
            o_ = ob[Tq % 2]
            x_ = xs[Tq % 2]
            for j in range(4):
                for half in range(2):
                    ps = PS[nps % 4]
                    nps += 1
                    for kc in range(8):
                        P.mm(ps.t[:, :], o_.t[:, kc, j * 128:(j + 1) * 128], wo.t[:, kc, half * 512:(half + 1) * 512],
                             kc == 0, kc == 7, [o_.r, wo.r], [ps.r])
                    P.tt("dve", x_.t[:, j, half * 512:(half + 1) * 512], ps.t[:, :], x_.t[:, j, half * 512:(half + 1) * 512],
                         ALU.add, [ps.r, x_.r], [x_.r])
            P.dma("sp", y[Tq * 512:(Tq + 1) * 512, :].rearrange("(j p) d -> p j d", p=128), x_.t[:], f"st_xs{Tq % 2}",
                  [x_.r], [Ry[Tq * 4 + j] for j in range(4)])
        P.barrier()
        P.reset(m)

    def phase_F(layer):
        m = P.mark()
        TT = 2
        W = TT * 128
        NS = S // W
        wup = T(P.alloc([128, 8, 2 * FFN], BF16, "wup"))
        for kc in range(8):
            P.dma("pool", wup.t[:, kc, :], ffn_w_up[layer, kc * 128:(kc + 1) * 128, :], "ld_wup", (), [wup.r])
        wd = T(P.alloc([128, 22, D], BF16, "wd"))
        P.dma("sp", wd.t[:], wd_s[layer], "ld_wd", [Rwd[layer]], [wd.r])
        xs = [T(P.alloc([128, TT, 1024], F32, "xs")) for _ in range(2)]
        nb = NormBufs(TT)
        hT = [T(P.alloc([128, 8, W + 2], BF16, "hT")) for _ in range(2)]
        gv = T(P.alloc([128, 22, W], BF16, "gv"))
        ya = [T(P.alloc([128, W], F32, "ya")) for _ in range(4)]
        yb = [T(P.alloc([128, W], F32, "yb")) for _ in range(4)]
        sg = [T(P.alloc([128, W], F32, "sg")) for _ in range(2)]

        def cw(ch, k):
            col = (layer * NCH + ch) * 4 + k
            return convw.t[:, col:col + 1]

        def load(Ti):
            x_ = xs[Ti % 2]
            P.dma("sp", x_.t[:], y[Ti * W:(Ti + 1) * W, :].rearrange("(j p) d -> p j d", p=128), f"ld_xs{Ti % 2}",
                  [Ry[Ti * TT + j] for j in range(TT)], [x_.r])

        load(0)
        P.memset("pool", hT[0].t[:, :, 0:2], 0.0, [hT[0].r])
        nu = 0
        nd = 0
        for Ti in range(NS):
            if Ti + 1 < NS:
                load(Ti + 1)
            x_ = xs[Ti % 2]
            h = hT[Ti % 2]
            if Ti > 0:
                hp = hT[(Ti - 1) % 2]
                P.copy("pool", h.t[:, :, 0:2], hp.t[:, :, W:W + 2], [hp.r], [h.r])
            emit_norm_T(nb, x_, h, 2, layer, 1)
            for mch in range(22):
                outs = []
                for side in range(2):
                    ch = mch + 22 * side
                    ps = PS[nu % 4]
                    a = ya[nu % 4]
                    b = yb[nu % 4]
                    nu += 1
                    for kc in range(8):
                        P.mm(ps.t[:, 0:W + 2], wup.t[:, kc, ch * 128:(ch + 1) * 128], h.t[:, kc, :], kc == 0, kc == 7,
                             [wup.r, h.r], [ps.r])
                    P.act(a.t[:], ps.t[:, 2:W + 2], AF.Identity, [ps.r, convw.r], [a.r], bias=cw(ch, 3), scale=cw(ch, 2))
                    P.stt("dve", b.t[:], ps.t[:, 1:W + 1], cw(ch, 1), a.t[:], ALU.mult, ALU.add, [ps.r, a.r, convw.r], [b.r])
                    P.stt("dve", a.t[:], ps.t[:, 0:W], cw(ch, 0), b.t[:], ALU.mult, ALU.add, [ps.r, b.r, convw.r], [a.r])
                    outs.append(a)
                s_ = sg[mch % 2]
                P.act(s_.t[:], outs[0].t[:], AF.Silu, [outs[0].r], [s_.r])
                P.tt("pool", gv.t[:, mch, :], s_.t[:], outs[1].t[:], ALU.mult, [s_.r, outs[1].r], [gv.r])
            for j in range(TT):
                for half in range(2):
                    ps = PS[4 + nd % 2]
                    nd += 1
                    for mch in range(22):
                        P.mm(ps.t[:, :], gv.t[:, mch, j * 128:(j + 1) * 128], wd.t[:, mch, half * 512:(half + 1) * 512],
                             mch == 0, mch == 21, [gv.r, wd.r], [ps.r])
                    P.tt("dve", x_.t[:, j, half * 512:(half + 1) * 512], ps.t[:, :], x_.t[:, j, half * 512:(half + 1) * 512],
                         ALU.add, [ps.r, x_.r], [x_.r])
            P.dma("sp", y[Ti * W:(Ti + 1) * W, :].rearrange("(j p) d -> p j d", p=128), x_.t[:], f"st_xs{Ti % 2}",
                  [x_.r], [Ry[Ti * TT + j] for j in range(TT)])
        P.barrier()
        P.reset(m)


    class L1:
        pass

    def alloc_L1():
        L = L1()
        L.knT = T(P.alloc([128, S], BF16, "knT"))
        L.vals = T(P.alloc([128, NT, 129], BF16, "vals"))
        L.kiT2 = T(P.alloc([128, S], BF16, "kiT2"))
        L.qiT = T(P.alloc([128, 4, S], BF16, "qiT"))
        L.wi = T(P.alloc([128, NT, 8], F32, "wi"))
        L.gq = T(P.alloc([128, 2], F32, "gq"))
        return L

    def phase_A1(L):
        m = P.mark()
        win = T(P.alloc([128, 8, 2760], BF16, "win1"))
        for kc in range(8):
            P.dma("pool", win.t[:, kc, :], dsa_w_in[kc * 128:(kc + 1) * 128, :], "ld_win", (), [win.r])
        wki2 = T(P.alloc([128, 8, 128], BF16, "wki2"))
        P.copy("dve", wki2.t[:, :, 0:64], win.t[:, :, 2688:2752], [win.r], [wki2.r])
        P.copy("dve", wki2.t[:, :, 64:128], win.t[:, :, 2688:2752], [win.r], [wki2.r])
        gT = T(P.alloc([128, 2], F32, "gT"))
        P.dma("sp", gT.t[:], dsa_gT[:, :], "ld_gT", (), [gT.r])
        P.ts("dve", L.gq.t[:, 0:1], gT.t[:, 0:1], 128.0 ** -0.5, None, ALU.mult, None, [gT.r], [L.gq.r])
        P.copy("dve", L.gq.t[:, 1:2], gT.t[:, 1:2], [gT.r], [L.gq.r])
        P.memset("pool", L.vals.t[:], 1.0, [L.vals.r])
        xs = [T(P.alloc([128, 4, 1024], F32, "xs")) for _ in range(2)]
        nb = NormBufs(4)
        hT = [T(P.alloc([128, 8, 512], BF16, "hT")) for _ in range(2)]
        qnb = T(P.alloc([128, 4, 16, 128], BF16, "qnb"))
        SQ = [T(P.alloc([128, 512], BF16, "sq")) for _ in range(2)]
        RS = [T(P.alloc([128, 512], F32, "rs")) for _ in range(2)]

        def load(Tq):
            xb = xs[Tq % 2]
            P.dma("sp", xb.t[:], y[Tq * 512:(Tq + 1) * 512, :].rearrange("(j p) d -> p j d", p=128),
                  f"ld_xs{Tq % 2}", [Ry[Tq * 4 + j] for j in range(4)], [xb.r])

        load(0)
        n = 0
        nev = 0
        for Tq in range(8):
            if Tq + 1 < 8:
                load(Tq + 1)
            xb = xs[Tq % 2]
            h = hT[Tq % 2]
            emit_norm_T(nb, xb, h, 0, 1, 0)
            for hd in range(17):
                col0 = hd * 128 if hd < 16 else 2048
                psq = PS[n % 2]
                pss = PS[2 + n % 2]
                sq = SQ[n % 2]
                rs = RS[n % 2]
                n += 1
                for kc in range(8):
                    P.mm(psq.t[:, :], win.t[:, kc, col0:col0 + 128], h.t[:, kc, :], kc == 0, kc == 7, [win.r, h.r], [psq.r])
                P.act(sq.t[:], psq.t[:, :], AF.Square, [psq.r], [sq.r])
                P.mm(pss.t[:, :], ones, sq.t[:], True, True, [consts.r, sq.r], [pss.r])
                P.act(rs.t[:], pss.t[:, :], AF.Sqrt, [pss.r], [rs.r], bias=EPS, scale=1.0 / 128)
                P.op("dve", lambda e, rs=rs: e.reciprocal(out=rs.t[:], in_=rs.t[:]), [rs.r], [rs.r])
                if hd < 16:
                    P.stt("dve", qnb.t[:, :, hd, :], psq.t[:, :].rearrange("p (j t) -> p j t", j=4), L.gq.t[:, 0:1],
                          rs.t[:].rearrange("p (j t) -> p j t", j=4), ALU.mult, ALU.mult, [psq.r, rs.r, L.gq.r], [qnb.r])
                else:
                    P.stt("dve", L.knT.t[:, Tq * 512:(Tq + 1) * 512], psq.t[:, :], L.gq.t[:, 1:2], rs.t[:], ALU.mult, ALU.mult,
                          [psq.r, rs.r, L.gq.r], [L.knT.r])
            for j in range(4):
                P.dma("sp", qn_s[Tq * 4 + j], qnb.t[:, j, :, :].rearrange("p h t -> p (h t)"), "st_qn", [qnb.r], [Rqn[Tq * 4 + j]])

            def evac(out, in_, reads, writes, scale=None):
                nonlocal nev
                if nev % 2 == 0:
                    if scale is None:
                        P.act(out, in_, AF.Identity, reads, writes)
                    else:
                        P.act(out, in_, AF.Identity, reads, writes, scale=scale)
                else:
                    if scale is None:
                        P.copy("dve", out, in_, reads, writes)
                    else:
                        P.ts("dve", out, in_, scale, None, ALU.mult, None, reads, writes)
                nev += 1

            for j in range(4):
                ps = PS[4]
                for kc in range(8):
                    P.mm(ps.t[:, 0:128], h.t[:, kc, j * 128:(j + 1) * 128], win.t[:, kc, 2048:2176], kc == 0, kc == 7,
                         [win.r, h.r], [ps.r])
                evac(L.vals.t[:, Tq * 4 + j, 0:128], ps.t[:, 0:128], [ps.r], [L.vals.r])
            for c in range(4):
                ps = PS[5]
                for kc in range(8):
                    P.mm(ps.t[:, :], win.t[:, kc, 2176 + c * 128: 2176 + (c + 1) * 128], h.t[:, kc, :], kc == 0, kc == 7,
                         [win.r, h.r], [ps.r])
                evac(L.qiT.t[:, c, Tq * 512:(Tq + 1) * 512], ps.t[:, :], [ps.r], [L.qiT.r])
            ps = PS[4]
            for kc in range(8):
                P.mm(ps.t[:, :], wki2.t[:, kc, :], h.t[:, kc, :], kc == 0, kc == 7, [wki2.r, h.r], [ps.r])
            evac(L.kiT2.t[:, Tq * 512:(Tq + 1) * 512], ps.t[:, :], [ps.r], [L.kiT2.r])
            for j in range(4):
                ps = PS[5]
                for kc in range(8):
                    P.mm(ps.t[:, 0:8], h.t[:, kc, j * 128:(j + 1) * 128], win.t[:, kc, 2752:2760], kc == 0, kc == 7,
                         [win.r, h.r], [ps.r])
                evac(L.wi.t[:, Tq * 4 + j, :], ps.t[:, 0:8], [ps.r], [L.wi.r], scale=8.0 ** -0.5)
        P.barrier()
        P.reset(m)

    def phase_B1(L, qts=range(NT)):
        m = P.mark()
        isc = [T(P.alloc([128, S], F32, "isc")) for _ in range(2)]
        pen = [T(P.alloc([128, S], BF16, "pen")) for _ in range(2)]
        dg = [T(P.alloc([128, 8, 128], BF16, "dg")) for _ in range(2)]
        Rh = [T(P.alloc([128, 512], BF16, "Rh")) for _ in range(8)]
        junk = T(P.alloc([128, S], BF16, "junkc"))
        thr = [T(P.alloc([128, 1], F32, "thr")) for _ in range(2)]
        cand = T(P.alloc([128, 1], F32, "cand"))
        ind = T(P.alloc([128, 1], F32, "ind"))
        cnt = [T(P.alloc([128, IDX_NIT], F32, "cnt")) for _ in range(2)]
        qnq = [T(P.alloc([128, 2048], BF16, "qnq")) for _ in range(2)]
        PTb = [T(P.alloc([128, 512], BF16, "PTb")) for _ in range(3)]
        olat = T(P.alloc([128, 2048], BF16, "olat"))
        olT = T(P.alloc([128, 2048], BF16, "olT"))
        oTq = T(P.alloc([128, 1024], BF16, "oTq"))
        rden = T(P.alloc([128, 16], F32, "rden"))
        xq = [T(P.alloc([128, 1024], F32, "xq")) for _ in range(2)]
        wo = T(P.alloc([128, 8, D], BF16, "wo1"))
        wuvP = T(P.alloc([128, 16, 128], BF16, "wuvP"))
        biasT = T(P.alloc([128, 2, 2048], BF16, "biasT"))
        P.dma("sp", wo.t[:], wo_s[1], "ld_wo", [Rwo[1]], [wo.r])
        P.memset("pool", wuvP.t[:], 0.0, [wuvP.r])
        for hd in range(16):
            c0 = (hd % 2) * 64
            P.dma("pool", wuvP.t[:, hd, c0:c0 + 64], dsa_w_uv[hd], "ld_wuv", (), [wuvP.r])
        P.dma("sp", isc[0].t[:], biasg_in[:, :], "ld_bg", (), [isc[0].r])
        P.dma("sp", isc[1].t[:], biasc_in[:, :], "ld_bc", (), [isc[1].r])
        P.tt("dve", biasT.t[:].rearrange("p w c -> p (w c)"), isc[0].t[:], isc[1].t[:], ALU.subtract, [isc[0].r, isc[1].r], [biasT.r])

        qtl = list(qts)
        ctr = {"z": 0, "ev": 0, "pt": 0}

        def ev2(out, in_, reads, writes, relu=False):
            k = ctr["ev"]
            ctr["ev"] += 1
            if k % 2 == 0:
                P.act(out, in_, AF.Relu if relu else AF.Identity, reads, writes)
            else:
                if relu:
                    P.ts("dve", out, in_, 0.0, None, ALU.max, None, reads, writes)
                else:
                    P.copy("dve", out, in_, reads, writes)

        def idx(qi_):
            qt = qtl[qi_]
            b = qi_ % 2
            nk = (qt + 1) * 128
            P.dma("sp", qnq[b].t[:], qn_s[qt], f"ld_qnq{b}", [Rqn[qt]], [qnq[b].r])
            P.dma("sp", xq[b].t[:], y[qt * 128:(qt + 1) * 128, :], f"ld_xq{b}", [Ry[qt]], [xq[b].r])
            for ih in range(8):
                P.ts("pool" if ih % 2 else "dve", dg[b].t[:, ih, :], ident, L.wi.t[:, qt, ih:ih + 1], None, ALU.mult, None,
                     [consts.r, L.wi.r], [dg[b].r])
            for ck in range((nk + 511) // 512):
                c_lo = ck * 512
                w = min(512, nk - c_lo)
                for ih in range(8):
                    pz = PS[ctr["z"] % 2]
                    ctr["z"] += 1
                    r0 = (ih % 2) * 64
                    P.mm(pz.t[:, 0:w], L.qiT.t[r0:r0 + 64, ih // 2, qt * 128:(qt + 1) * 128], L.kiT2.t[r0:r0 + 64, c_lo:c_lo + w],
                         True, True, [L.qiT.r, L.kiT2.r], [pz.r])
                    ev2(Rh[ih].t[:, 0:w], pz.t[:, 0:w], [pz.r], [Rh[ih].r], relu=True)
                pi = PS[2]
                for ih in range(8):
                    P.mm(pi.t[:, 0:w], dg[b].t[:, ih, :], Rh[ih].t[:, 0:w], ih == 0, ih == 7, [dg[b].r, Rh[ih].r], [pi.r])
                ev2(isc[b].t[:, c_lo:c_lo + w], pi.t[:, 0:w], [pi.r], [isc[b].r])
            P.tt("dve", isc[b].t[:, nk - 128:nk], isc[b].t[:, nk - 128:nk], cmask, ALU.add, [isc[b].r, consts.r], [isc[b].r])
            P.memset("pool", thr[b].t[:], -IDX_LIM, [thr[b].r])
            if qt >= 2:
                P.memset("pool", cnt[b].t[:], 0.0, [cnt[b].r])
                for it in range(IDX_NIT):
                    step = IDX_LIM / (2.0 ** it)
                    P.ts("dve", cand.t[:], thr[b].t[:], step, None, ALU.add, None, [thr[b].r], [cand.r])
                    P.ts("dve", junk.t[:, 0:nk], isc[b].t[:, 0:nk], cand.t[:, 0:1], 0.0, ALU.is_ge, ALU.add,
                         [isc[b].r, cand.r], [junk.r, cnt[b].r], accum_out=cnt[b].t[:, it:it + 1])
                    P.ts("dve", ind.t[:], cnt[b].t[:, it:it + 1], 255.5, step, ALU.is_ge, ALU.mult, [cnt[b].r], [ind.r])
                    P.tt("dve", thr[b].t[:], thr[b].t[:], ind.t[:], ALU.add, [thr[b].r, ind.r], [thr[b].r])
            P.ts("pool", pen[b].t[:, 0:nk], isc[b].t[:, 0:nk], thr[b].t[:, 0:1], -30000.0, ALU.is_lt, ALU.mult,
                 [isc[b].r, thr[b].r], [pen[b].r])

        def att(qi_):
            qt = qtl[qi_]
            b = qi_ % 2
            for hp in range(2):
                touched = set()
                steps = [(kb, g) for kb in range(qt + 1) for g in range(2)]

                def S_(k, hp=hp, steps=steps):
                    kb, g = steps[k]
                    near = kb >= qt - 1
                    hh0 = hp * 8 + g * 4
                    ps = PS[(ctr["pt"] + k) % 2]
                    P.mm(ps.t[:, :], L.knT.t[:, kb * 128:(kb + 1) * 128], qnq[b].t[:, hh0 * 128:(hh0 + 4) * 128], True, False,
                         [L.knT.r, qnq[b].r], [ps.r])
                    P.mm(ps.t[:, :], pen[b].t[:, kb * 128:(kb + 1) * 128], irep4, False, not near, [pen[b].r, consts.r], [ps.r])
                    if near:
                        P.mm(ps.t[:, :], ident, biasT.t[:, 0 if kb == qt else 1, hh0 * 128:(hh0 + 4) * 128], False, True,
                             [consts.r, biasT.r], [ps.r])
                    pt = PTb[(ctr["pt"] + k) % 3]
                    P.act(pt.t[:], ps.t[:, :], AF.Exp, [ps.r], [pt.r])

                def V_(k, hp=hp, steps=steps, touched=touched):
                    kb, g = steps[k]
                    pt = PTb[(ctr["pt"] + k) % 3]
                    for i4 in range(4):
                        hd = g * 4 + i4
                        bk = 3 + hd // 3
                        off = (hd % 3) * 129
                        P.mm(PS[bk].t[:, off:off + 129], pt.t[:, i4 * 128:(i4 + 1) * 128], L.vals.t[:, kb, :], bk not in touched, False,
                             [pt.r, L.vals.r], [PS[bk].r])
                        touched.add(bk)

                S_(0)
                for k in range(len(steps)):
                    if k + 1 < len(steps):
                        S_(k + 1)
                    V_(k)
                ctr["pt"] += len(steps)
                for bi in range(3):
                    nh = 3 if bi < 2 else 2
                    P.op("dve", lambda e, bi=bi, nh=nh, hp=hp: e.reciprocal(
                        out=rden.t[:, hp * 8 + bi * 3: hp * 8 + bi * 3 + nh],
                        in_=PS[3 + bi].t[:, 0:nh * 129].rearrange("p (h c) -> p h c", c=129)[:, :, 128]),
                        [PS[3 + bi].r], [rden.r])
                for hd in range(8):
                    bk = 3 + hd // 3
                    off = (hd % 3) * 129
                    gh = hp * 8 + hd
                    if hd % 2 == 0:
                        P.act(olat.t[:, gh * 128:(gh + 1) * 128], PS[bk].t[:, off:off + 128], AF.Identity, [PS[bk].r, rden.r], [olat.r],
                              scale=rden.t[:, gh:gh + 1])
                    else:
                        P.ts("dve", olat.t[:, gh * 128:(gh + 1) * 128], PS[bk].t[:, off:off + 128], rden.t[:, gh:gh + 1], None,
                             ALU.mult, None, [PS[bk].r, rden.r], [olat.r])
            for hd in range(16):
                P.tr(PT[hd // 8].t[:, (hd % 8) * 128:(hd % 8 + 1) * 128], olat.t[:, hd * 128:(hd + 1) * 128], ident,
                     [olat.r, consts.r], [PT[hd // 8].r])
            P.copy("dve", olT.t[:, 0:1024], PT[0].t[:, :], [PT[0].r], [olT.r])
            P.act(olT.t[:, 1024:2048], PT[1].t[:, :], AF.Identity, [PT[1].r], [olT.r])
            for half in range(2):
                bank = PS[2]
                for c4 in range(4):
                    j2 = half * 4 + c4
                    P.mm(bank.t[:, c4 * 128:(c4 + 1) * 128], wuvP.t[:, 2 * j2, :], olT.t[:, (2 * j2) * 128:(2 * j2 + 1) * 128],
                         c4 == 0, False, [wuvP.r, olT.r], [bank.r])
                    P.mm(bank.t[:, c4 * 128:(c4 + 1) * 128], wuvP.t[:, 2 * j2 + 1, :], olT.t[:, (2 * j2 + 1) * 128:(2 * j2 + 2) * 128],
                         False, True, [wuvP.r, olT.r], [bank.r])
                ev2(oTq.t[:, half * 512:(half + 1) * 512], bank.t[:, :], [bank.r], [oTq.r])
            for half in range(2):
                bank = PS[half]
                for c in range(8):
                    P.mm(bank.t[:, :], oTq.t[:, c * 128:(c + 1) * 128], wo.t[:, c, half * 512:(half + 1) * 512], c == 0, c == 7,
                         [oTq.r, wo.r], [bank.r])
                P.tt("dve", xq[b].t[:, half * 512:(half + 1) * 512], bank.t[:, :], xq[b].t[:, half * 512:(half + 1) * 512], ALU.add,
                     [bank.r, xq[b].r], [xq[b].r])
            P.dma("sp", y[qt * 128:(qt + 1) * 128, :], xq[b].t[:], f"st_xq{b}", [xq[b].r], [Ry[qt]])

        idx(0)
        for qi_ in range(len(qtl)):
            if qi_ + 1 < len(qtl):
                idx(qi_ + 1)
            att(qi_)
        P.barrier()
        P.reset(m)

    order = ["P", "A0", "B0", "C0", "F0", "A1", "B1", "F1"]
    upto = order.index(stop) if stop is not None else len(order) - 1
    if n_layers == 1:
        upto = min(upto, order.index("F0"))
    phase_P()
    if dbg:
        dbg_mod = nc.dram_tensor("dbg_mod", [128, 128], F32, kind="ExternalOutput").ap()
        P.dma("sp", dbg_mod[:, 0:96], modfm.t[:], "st_dbg0", [modfm.r], [])
        P.dma("sp", dbg_mod[:, 96:128], gs.t[:], "st_dbg1", [gs.r], [])
    P.barrier()
    P.reset(pers_mark)
    if upto >= 1:
        phase_A0()
    if upto >= 2:
        phase_B0()
    if upto >= 3:
        phase_C(0)
    if upto >= 4:
        phase_F(0)
    if upto >= 5:
        L = alloc_L1()
        phase_A1(L)
    if upto >= 6:
        phase_B1(L)
        P.reset(pers_mark)
    if upto >= 7:
        phase_F(1)
    P.barrier()
    P.emit()
    return nc, P


def make_consts():
    c = np.zeros((128, NCONST), np.float32)
    p = np.arange(128)[:, None]
    q = np.arange(128)[None, :]
    c[:, C_ID:C_ID + 128] = (p == q)
    c[:, C_TRI:C_TRI + 128] = (p < q)
    c[:, C_NEGM:C_NEGM + 128] = np.where(p >= q, -30000.0, 0.0)
    c[:, C_NEGU:C_NEGU + 128] = np.where(p >= q, -1.0, 0.0)
    c[:, C_ONES:C_ONES + 128] = 1.0
    c[:, C_NEGI:C_NEGI + 128] = -1.0 * (p == q)
    c[:, C_CMASK:C_CMASK + 128] = np.where(q > p, -1e30, 0.0)
    for r_ in range(4):
        c[:, C_IREP + r_ * 128:C_IREP + (r_ + 1) * 128] = (p == q)
    return c.astype(ml_dtypes.bfloat16)


def make_consts2():
    osel = np.zeros((128, 32, 128), np.float32)
    for i in range(32):
        osel[:, i, i] = 1.0
    ltm = np.zeros((128, 96), np.float32)
    k = np.arange(32)[:, None]
    i_ = np.arange(32)[None, :]
    ltm[0:32, 64:96] = (k > i_)
    nsel = np.zeros((32, 32, 128), np.float32)
    for i in range(32):
        nsel[i, i, :] = -1.0
    bf = ml_dtypes.bfloat16
    return osel.reshape(128, 4096).astype(bf), ltm.astype(bf), nsel.reshape(32, 4096).astype(bf)


def t5_bucket_np(dist):
    n = np.maximum(dist, 0)
    nf = np.maximum(n, 1).astype(np.float32)
    large = 16 + (np.log(nf / 16) / np.float32(math.log(128 / 16)) * 16).astype(np.int32)
    large = np.minimum(large, 31)
    return np.where(n < 16, n, large)


def prep_inputs(inp, cores):
    f = lambda a: np.ascontiguousarray(np.asarray(a, dtype=np.float32))
    x = f(inp["x"])
    c = f(inp["c"])
    ada_w = f(inp["ada_w"])
    ada_b = f(inp["ada_b"])
    ada_bT = np.ascontiguousarray(ada_b.reshape(2, 48, 128).transpose(2, 0, 1).reshape(128, 96))
    nmixT = np.ascontiguousarray(f(inp["norm_mix"]).reshape(2, 8, 128).transpose(2, 0, 1).reshape(128, 16))
    nffnT = np.ascontiguousarray(f(inp["norm_ffn"]).reshape(2, 8, 128).transpose(2, 0, 1).reshape(128, 16))
    cw = f(inp["ffn_conv_w"])
    cb = f(inp["ffn_conv_b"])
    conv = np.concatenate([cw, cb[:, None, :]], axis=1)
    convT = np.ascontiguousarray(conv.reshape(2, 4, NCH, 128).transpose(3, 0, 2, 1).reshape(128, 2 * NCH * 4))
    rb = f(inp["rel_bias"])
    p_ = np.arange(128)[:, None]
    q_ = np.arange(128)[None, :]
    bg = np.zeros((128, 2, 16, 128), np.float32)
    for w_ in range(2):
        bk = t5_bucket_np(w_ * 128 + q_ - p_)
        bg[:, w_, :, :] = rb[bk].transpose(0, 2, 1)
    bc = np.broadcast_to(rb[31][None, None, :, None], (128, 2, 16, 128))
    osel_c, sm_c, nsel_c = make_consts2()
    shared = {
        "osel": osel_c, "smc": sm_c, "nsel": nsel_c,
        "ada_w": ada_w, "ada_b": ada_b, "ada_bT": ada_bT, "nmixT": nmixT, "nffnT": nffnT,
        "sb_w_in": f(inp["sb_w_in"])[0], "sb_w_out": f(inp["sb_w_out"])[0], "dsa_w_out": f(inp["dsa_w_out"])[0],
        "ffn_w_up": f(inp["ffn_w_up"]), "ffn_w_down": f(inp["ffn_w_down"]),
        "convT": convT, "consts": make_consts(),
        "dsa_w_in": f(inp["dsa_w_in"])[0], "dsa_w_uv": f(inp["dsa_w_uv"])[0],
        "dsa_gT": np.ascontiguousarray(np.stack([f(inp["dsa_q_norm"])[0], f(inp["dsa_k_norm"])[0]], axis=1)),
        "biasg": np.ascontiguousarray(bg.reshape(128, 4096)), "biasc": np.ascontiguousarray(bc.reshape(128, 4096)),
    }
    maps = []
    for b in cores:
        d = dict(shared)
        d["x"] = x[b]
        d["cT"] = np.ascontiguousarray(c[b].reshape(8, 128).T)
        maps.append(d)
    return maps


_CACHE = {}


def kernel(**inputs):
    if "nc" not in _CACHE:
        _CACHE["nc"] = build()[0]
    nc = _CACHE["nc"]
    maps = prep_inputs(inputs, range(8))
    res = run_bass_kernel_spmd(nc, maps, core_ids=list(range(8)))
    out = np.stack([np.asarray(r["y"]) for r in res.results], axis=0)
    return out.astype(np.float32)
```
